# Optimizing a Trainium2 kernel written in Bass

```python
import math
import jax, jax.numpy as jnp
from jax import lax
import numpy as np

D_MODEL = 1024
BATCH = 4
SEQ = 4096
DEPTH = 2

D_MIX = 2 * D_MODEL
EPS = 1e-6

A_WIDTH = D_MIX // 2
A_HEAD_DIM = 64
A_Q_HEADS = A_WIDTH // A_HEAD_DIM
A_KV_HEADS = 4
A_GROUP = A_Q_HEADS // A_KV_HEADS
A_KV_WIDTH = A_KV_HEADS * A_HEAD_DIM
WINDOW = 128
A_BLOCK = 128

B_WIDTH = D_MIX // 4
B_HEADS = 4
B_DK = B_WIDTH // (2 * B_HEADS)
B_DV = B_WIDTH // B_HEADS
B_QK_WIDTH = B_HEADS * B_DK
B_GATE_RANK = 16
B_GATE_TAU = 16.0
B_CHUNK = 16

C_WIDTH = D_MIX // 4
C_GROUP_CH = 16
C_GROUPS = C_WIDTH // C_GROUP_CH
C_STATE = 64

PROJ_SIZES = (A_WIDTH, A_KV_WIDTH, A_KV_WIDTH, A_WIDTH,
              B_QK_WIDTH, B_QK_WIDTH, B_WIDTH, B_GATE_RANK, B_WIDTH,
              C_WIDTH, C_WIDTH)
PROJ_COLS = sum(PROJ_SIZES)

kernel_name = "hybrid_swa_gla_s5_parallel_heads"


def rmsnorm(x, g):
    xf = x.astype(jnp.float32)
    y = xf * lax.rsqrt(jnp.mean(xf * xf, axis=-1, keepdims=True) + EPS)
    return (y * g.astype(jnp.float32)).astype(x.dtype)


def alibi_slopes(n):
    return jnp.exp2(-8.0 * jnp.arange(1, n + 1, dtype=jnp.float32) / n)


def sliding_window_attention(q, k, v, sinks):
    bsz, s_len = q.shape[:2]
    nb = s_len // A_BLOCK
    q = q.reshape(bsz, nb, A_BLOCK, A_KV_HEADS, A_GROUP, A_HEAD_DIM)
    k = k.reshape(bsz, s_len, A_KV_HEADS, A_HEAD_DIM)
    v = v.reshape(bsz, s_len, A_KV_HEADS, A_HEAD_DIM)
    pad = ((0, 0), (A_BLOCK, 0), (0, 0), (0, 0))
    blk_shape = (bsz, nb, A_BLOCK, A_KV_HEADS, A_HEAD_DIM)
    kk = jnp.concatenate([jnp.pad(k, pad)[:, :s_len].reshape(blk_shape), k.reshape(blk_shape)], axis=2)
    vv = jnp.concatenate([jnp.pad(v, pad)[:, :s_len].reshape(blk_shape), v.reshape(blk_shape)], axis=2)
    s = jnp.einsum('bnqkgd,bnskd->bnkgqs', q, kk).astype(jnp.float32) * (A_HEAD_DIM ** -0.5)
    i = jnp.arange(A_BLOCK)[:, None]
    j = jnp.arange(2 * A_BLOCK)[None, :]
    dist = i + A_BLOCK - j
    key_pos = jnp.arange(nb)[:, None, None] * A_BLOCK - A_BLOCK + j[None]
    valid = (dist >= 0)[None] & (dist < WINDOW)[None] & (key_pos >= 0)
    slopes = alibi_slopes(A_Q_HEADS).reshape(A_KV_HEADS, A_GROUP)
    s = s - slopes[:, :, None, None] * dist.astype(jnp.float32)
    s = jnp.where(valid[None, :, None, None], s, -jnp.inf)
    sink = sinks.astype(jnp.float32).reshape(A_KV_HEADS, A_GROUP)[:, :, None, None]
    m = jnp.maximum(jnp.max(s, axis=-1, keepdims=True), sink)
    p = jnp.exp(s - m)
    probs = (p / (jnp.sum(p, axis=-1, keepdims=True) + jnp.exp(sink - m))).astype(vv.dtype)
    o = jnp.einsum('bnkgqs,bnskd->bnqkgd', probs, vv)
    return o.reshape(bsz, s_len, A_WIDTH)


def gated_linear_attention(q, k, v, log_a):
    bsz, s_len = q.shape[:2]
    nc = s_len // B_CHUNK
    cshape = (bsz, nc, B_CHUNK, B_HEADS)
    q = (q.astype(jnp.float32) * (B_DK ** -0.5)).reshape(cshape + (B_DK,))
    k = k.astype(jnp.float32).reshape(cshape + (B_DK,))
    v = v.astype(jnp.float32).reshape(cshape + (B_DV,))
    b = jnp.cumsum(log_a.astype(jnp.float32).reshape(cshape + (B_DK,)), axis=2)
    causal = jnp.tril(jnp.ones((B_CHUNK, B_CHUNK), dtype=bool))[None, None, :, :, None, None]
    decay = jnp.exp(jnp.where(causal, b[:, :, :, None] - b[:, :, None, :], -jnp.inf))
    attn = jnp.einsum('bnihd,bnjhd,bnijhd->bnhij', q, k, decay)
    o_intra = jnp.einsum('bnhij,bnjhv->bnihv', attn, v)
    b_last = b[:, :, -1]
    u = jnp.einsum('bnjhd,bnjhv->bnhdv', k * jnp.exp(b_last[:, :, None] - b), v)
    chunk_decay = jnp.exp(b_last)

    def step(state, inp):
        dec, uu = inp
        return dec[..., None] * state + uu, state

    init = jnp.zeros((bsz, B_HEADS, B_DK, B_DV), jnp.float32)
    _, s_prev = lax.scan(step, init, (jnp.moveaxis(chunk_decay, 1, 0), jnp.moveaxis(u, 1, 0)))
    s_prev = jnp.moveaxis(s_prev, 0, 1)
    o_inter = jnp.einsum('bnihd,bnhdv->bnihv', q * jnp.exp(b), s_prev)
    return (o_intra + o_inter).reshape(bsz, s_len, B_HEADS, B_DV)


def _complex_affine_combine(e1, e2):
    a1r, a1i, b1r, b1i = e1
    a2r, a2i, b2r, b2i = e2
    return (a2r * a1r - a2i * a1i,
            a2r * a1i + a2i * a1r,
            a2r * b1r - a2i * b1i + b2r,
            a2r * b1i + a2i * b1r + b2i)


def s5_ssm(u, a_re, a_im, log_dt, b_re, b_im, c_re, c_im, d):
    bsz, s_len = u.shape[:2]
    uf = u.astype(jnp.float32).reshape(bsz, s_len, C_GROUPS, C_GROUP_CH)
    ar = a_re.astype(jnp.float32)
    ai = a_im.astype(jnp.float32)
    dt = jnp.exp(log_dt.astype(jnp.float32))[:, None]
    mag = jnp.exp(ar * dt)
    abar_re = mag * jnp.cos(ai * dt)
    abar_im = mag * jnp.sin(ai * dt)
    den = ar * ar + ai * ai
    num_re = abar_re - 1.0
    f_re = (num_re * ar + abar_im * ai) / den
    f_im = (abar_im * ar - num_re * ai) / den
    bu_re = jnp.einsum('blgh,gph->blgp', uf, b_re.astype(jnp.float32))
    bu_im = jnp.einsum('blgh,gph->blgp', uf, b_im.astype(jnp.float32))
    in_re = f_re * bu_re - f_im * bu_im
    in_im = f_re * bu_im + f_im * bu_re
    _, _, h_re, h_im = lax.associative_scan(
        _complex_affine_combine,
        (jnp.broadcast_to(abar_re, in_re.shape), jnp.broadcast_to(abar_im, in_re.shape), in_re, in_im),
        axis=1)
    y = (jnp.einsum('blgp,ghp->blgh', h_re, c_re.astype(jnp.float32))
         - jnp.einsum('blgp,ghp->blgh', h_im, c_im.astype(jnp.float32)))
    return y.reshape(bsz, s_len, C_WIDTH) + d.astype(jnp.float32) * uf.reshape(bsz, s_len, C_WIDTH)


def hybrid_layer(x, c, w_mod, b_mod, g_pre, g_post, w_in, sinks, w_alpha, b_alpha, g_gla,
                 a_re, a_im, log_dt, b_re, b_im, c_re, c_im, d, w_glu, b_glu, w_out):
    bsz, s_len = x.shape[:2]
    mod = jax.nn.silu(c) @ w_mod + b_mod
    shift, scale, gate = jnp.split(mod, 3, axis=-1)
    h = rmsnorm(x, g_pre) * (1.0 + scale[:, None]) + shift[:, None]
    proj = h @ w_in
    pieces = []
    off = 0
    for size in PROJ_SIZES:
        pieces.append(proj[..., off:off + size])
        off += size
    a_q, a_k, a_v, a_g, b_q, b_k, b_v, b_lr, b_g, c_u, c_g = pieces

    o_a = sliding_window_attention(a_q, a_k, a_v, sinks) * jax.nn.silu(a_g)

    gate_logits = (b_lr @ w_alpha + b_alpha).astype(jnp.float32)
    log_a = (jax.nn.log_sigmoid(gate_logits) / B_GATE_TAU).reshape(bsz, s_len, B_HEADS, B_DK)
    o_b = gated_linear_attention(b_q.reshape(bsz, s_len, B_HEADS, B_DK),
                                 b_k.reshape(bsz, s_len, B_HEADS, B_DK),
                                 b_v.reshape(bsz, s_len, B_HEADS, B_DV), log_a)
    o_b = rmsnorm(o_b, g_gla.reshape(B_HEADS, B_DV)).reshape(bsz, s_len, B_WIDTH)
    o_b = o_b.astype(x.dtype) * jax.nn.silu(b_g)

    y = jax.nn.gelu(s5_ssm(c_u, a_re, a_im, log_dt, b_re, b_im, c_re, c_im, d)).astype(x.dtype)
    y = y * jax.nn.sigmoid(y @ w_glu + b_glu)
    o_c = y * jax.nn.silu(c_g)

    mix = jnp.concatenate([o_a, o_b, o_c], axis=-1)
    out = rmsnorm(mix @ w_out, g_post)
    return x + gate[:, None] * out


def setup_inputs(seed: int = 0) -> dict:
    key = jax.random.key(seed)
    ks = jax.random.split(key, 24)
    L, D = DEPTH, D_MODEL
    nrm = lambda k, shape, s: jax.random.normal(k, shape, jnp.float32) * s
    n_idx = jnp.arange(C_STATE, dtype=jnp.float32)
    return {
        "x": nrm(ks[0], (BATCH, SEQ, D), 1.0),
        "c": nrm(ks[1], (BATCH, D), 1.0),
        "w_mod": nrm(ks[2], (L, D, 3 * D), D ** -0.5),
        "b_mod": nrm(ks[3], (L, 3 * D), 0.02),
        "g_pre": 1.0 + nrm(ks[4], (L, D), 0.02),
        "g_post": 1.0 + nrm(ks[5], (L, D), 0.02),
        "w_in": nrm(ks[6], (L, D, PROJ_COLS), D ** -0.5),
        "attn_sinks": nrm(ks[7], (L, A_Q_HEADS), 0.5),
        "gla_w_alpha": nrm(ks[8], (L, B_GATE_RANK, B_QK_WIDTH), B_GATE_RANK ** -0.5),
        "gla_b_alpha": nrm(ks[9], (L, B_QK_WIDTH), 0.1),
        "gla_norm_g": 1.0 + nrm(ks[10], (L, B_WIDTH), 0.02),
        "s5_a_re": -0.5 + nrm(ks[11], (L, C_GROUPS, C_STATE), 0.01),
        "s5_a_im": math.pi * n_idx + nrm(ks[12], (L, C_GROUPS, C_STATE), 0.01),
        "s5_log_dt": jax.random.uniform(ks[13], (L, C_GROUPS), jnp.float32, math.log(1e-3), math.log(1e-1)),
        "s5_b_re": nrm(ks[14], (L, C_GROUPS, C_STATE, C_GROUP_CH), (2 * C_GROUP_CH) ** -0.5),
        "s5_b_im": nrm(ks[15], (L, C_GROUPS, C_STATE, C_GROUP_CH), (2 * C_GROUP_CH) ** -0.5),
        "s5_c_re": nrm(ks[16], (L, C_GROUPS, C_GROUP_CH, C_STATE), (2 * C_STATE) ** -0.5),
        "s5_c_im": nrm(ks[17], (L, C_GROUPS, C_GROUP_CH, C_STATE), (2 * C_STATE) ** -0.5),
        "s5_d": nrm(ks[18], (L, C_WIDTH), 1.0),
        "s5_w_glu": nrm(ks[19], (L, C_WIDTH, C_WIDTH), C_WIDTH ** -0.5),
        "s5_b_glu": nrm(ks[20], (L, C_WIDTH), 0.02),
        "w_out": nrm(ks[21], (L, D_MIX, D), D_MIX ** -0.5),
    }


def reference(x, c, w_mod, b_mod, g_pre, g_post, w_in, attn_sinks, gla_w_alpha, gla_b_alpha,
              gla_norm_g, s5_a_re, s5_a_im, s5_log_dt, s5_b_re, s5_b_im, s5_c_re, s5_c_im,
              s5_d, s5_w_glu, s5_b_glu, w_out):
    for l in range(DEPTH):
        x = hybrid_layer(x, c, w_mod[l], b_mod[l], g_pre[l], g_post[l], w_in[l], attn_sinks[l],
                         gla_w_alpha[l], gla_b_alpha[l], gla_norm_g[l],
                         s5_a_re[l], s5_a_im[l], s5_log_dt[l], s5_b_re[l], s5_b_im[l],
                         s5_c_re[l], s5_c_im[l], s5_d[l], s5_w_glu[l], s5_b_glu[l], w_out[l])
    return x
```

```python
import math
from contextlib import ExitStack
import numpy as np
import concourse.bass as bass
import concourse.mybir as mybir
from concourse.bass_utils import run_bass_kernel_spmd

F32 = mybir.dt.float32
BF16 = mybir.dt.bfloat16
I32 = mybir.dt.int32
ALU = mybir.AluOpType
AF = mybir.ActivationFunctionType

D = 1024
SEQ = 4096
NCOL = 5136
O_AQ, O_AK, O_AG, O_BQ, O_BK, O_BG, O_CU, O_CG, O_AV, O_BV, O_LR = (
    0, 1024, 1280, 2304, 2560, 2816, 3328, 3840, 4352, 4608, 5120)
EPS = 1e-6
TWO_PI = 2.0 * math.pi


def swa_tile_heads(j):
    return (j, 4 + j) if j < 4 else (8 + (j - 4), 12 + (j - 4))


class Buf:
    __slots__ = ("w", "rs", "dsem", "dcount", "name")

    def __init__(self, name=""):
        self.w = None
        self.rs = []
        self.dsem = None
        self.dcount = 0
        self.name = name


class Eng:
    def __init__(self, eng, sem, is_pe=False):
        self.eng = eng
        self.sem = sem
        self.count = 0
        self.known = {}
        self.is_pe = is_pe
        self.pending = False


class Prog:
    def __init__(self, nc, es):
        self.nc = nc
        self.es = es
        E = es.enter_context
        self.pe = Eng(nc.tensor, E(nc.semaphore("s_pe")), True)
        self.act = Eng(nc.scalar, E(nc.semaphore("s_act")))
        self.dve = Eng(nc.vector, E(nc.semaphore("s_dve")))
        self.pool = Eng(nc.gpsimd, E(nc.semaphore("s_pool")))
        self.sp = Eng(nc.sync, E(nc.semaphore("s_sp")))
        self.nsem = 0

    def _wait(self, e, deps):
        for (sem, val, owner) in deps:
            if owner is e:
                if e.is_pe or owner is self.sp:
                    continue
                if val < e.count:
                    continue
            k = id(sem)
            if e.known.get(k, 0) >= val:
                continue
            e.eng.wait_ge(sem, val)
            e.known[k] = val

    def _deps(self, reads, writes):
        deps = []
        for b in reads:
            if b.w is not None:
                deps.append(b.w)
        for b in writes:
            if b.w is not None:
                deps.append(b.w)
            deps.extend(b.rs)
        return deps

    def op(self, e, fn, reads=(), writes=(), inc=True):
        self._wait(e, self._deps(reads, writes))
        ins = fn(e.eng)
        tag_val = e.count + 1
        if inc:
            ins.then_inc(e.sem, 1)
            e.count += 1
        tag = (e.sem, tag_val, e)
        for b in writes:
            b.w = tag
            b.rs = []
        for b in reads:
            b.rs.append(tag)
            if len(b.rs) > 24:
                b.rs = b.rs[-24:]
        return ins

    def dma(self, q, out_ap, in_ap, reads=(), writes=(), tagbuf=None):
        self._wait(q, self._deps(reads, writes))
        tb = tagbuf
        if tb.dsem is None:
            tb.dsem = self.es.enter_context(self.nc.semaphore("d%d" % self.nsem))
            self.nsem += 1
        q.eng.dma_start(out=out_ap, in_=in_ap).then_inc(tb.dsem, 16)
        tb.dcount += 16
        tag = (tb.dsem, tb.dcount, None)
        for b in writes:
            b.w = tag
            b.rs = []
        for b in reads:
            b.rs.append(tag)


def _coll(P, q, groups, out_ap, in_ap, reads, writes, tagbuf):
    P._wait(q, P._deps(reads, writes))
    tb = tagbuf
    if tb.dsem is None:
        tb.dsem = P.es.enter_context(P.nc.semaphore("c%d" % P.nsem))
        P.nsem += 1
    q.eng.collective_compute("AllGather", ALU.bypass, replica_groups=groups, ins=[in_ap], outs=[out_ap]).then_inc(tb.dsem, 1)
    tb.dcount += 1
    tag = (tb.dsem, tb.dcount, None)
    for b in writes:
        b.w = tag
        b.rs = []
    for b in reads:
        b.rs.append(tag)


class _Stop(Exception):
    pass


STOP = 0
INTERLEAVE = True


LAG = 2


def build(T, NCORES=8, dbg=False):
    NL = 1
    NB = T // 128 + LAG
    TT_ = NB * 128
    groups = [[2 * i, 2 * i + 1] for i in range(NCORES // 2)]
    nc = bass.Bass("TRN2", target_bir_lowering=False)
    dr = {}

    def din(name, shape):
        dr[name] = nc.dram_tensor(name, list(shape), F32, kind="ExternalInput").ap()
        return dr[name]

    xT = din("xT", [D, TT_])
    flg = din("flg", [128, 2])
    cT = din("cT", [128, 8])
    wmod = din("wmod", [NL, D, 3 * D])
    bmod = din("bmod", [NL, 1, 3 * D])
    gpre = din("gpre", [NL, 128, 8])
    gpost = din("gpost", [NL, 128, 8])
    win = din("win", [NL, D, NCOL])
    wout = din("wout", [NL, 2 * D, D])
    sinks = din("sinks", [NL, 128, 8])
    walpha = din("walpha", [NL, 16, 256])
    balpha = din("balpha", [NL, 1, 256])
    ggla = din("ggla", [NL, 128, 4])
    s5col = din("s5col", [NL, 128, 3, 16])
    s5rep = din("s5rep", [NL, 128, 3, 2048])
    btpad = din("btpad", [NL, 2, 128, 2048])
    cpad = din("cpad", [NL, 2, 128, 2048])
    s5d = din("s5d", [NL, 128, 4])
    wglu = din("wglu", [NL, 512, 512])
    bglu = din("bglu", [NL, 128, 4])
    c_triu = din("c_triu", [128, 128])
    c_trils = din("c_trils", [128, 128])
    c_masku4 = din("c_masku4", [128, 512])
    c_swam = din("c_swam", [128, 8 * 4 * 128])
    c_iota = din("c_iota", [128, 128])
    c_onesz = din("c_onesz", [128, 128])
    outT = nc.dram_tensor("outT", [D, TT_], F32, kind="ExternalOutput").ap()
    x1T = None
    sndL = [nc.dram_tensor("snd%d" % i, [D, 128], F32, kind="Internal", addr_space="Local").ap() for i in range(3)]
    rcvL = [nc.dram_tensor("rcv%d" % i, [2 * D, 128], F32, kind="Internal", addr_space="Local").ap() for i in range(3)]
    wbfL = [nc.dram_tensor("wbf%d" % i, [40, 128, 8, 128], BF16, kind="Internal").ap() for i in range(NL)]
    wlrL = [nc.dram_tensor("wlr%d" % i, [128, 8, 16], BF16, kind="Internal").ap() for i in range(NL)]
    wobfL = [nc.dram_tensor("wobf%d" % i, [8, 128, 16, 128], BF16, kind="Internal").ap() for i in range(NL)]

    with ExitStack() as es:
        E = es.enter_context
        P = Prog(nc, es)
        pe, act, dve, pool, sp = P.pe, P.act, P.dve, P.pool, P.sp

        def sb(name, shape, dt=F32):
            return E(nc.sbuf_tensor(name, list(shape), dt))

        NWS = 5
        Wr = [sb("Wr%d" % i, [128, 8, 128], BF16) for i in range(NWS)]; bWr = [Buf("Wr%d" % i) for i in range(NWS)]
        Wbig = sb("Wbig", [128, 4, 8, 128], BF16); bWbig = Buf("Wbig")
        WO = sb("WO", [128, 16384], BF16); bWO = Buf("WO")
        WOf = WO[:].bitcast(F32); WOi = WO[:].bitcast(I32)
        WO4 = WO[:].rearrange("p (o k c) -> p o k c", o=8, k=16)
        bWdL = [Buf("wbf_dram%d" % i) for i in range(NL)]; bWOdL = [Buf("wobf_dram%d" % i) for i in range(NL)]; wctr = [0, 0, 0]
        CW = 640
        cst = [WOf[:, 4608 + i * 640:4608 + (i + 1) * 640] for i in range(2)]; bCst = [Buf("cst0"), Buf("cst1")]
        csb = [WO[:, 12288 + i * 640:12288 + (i + 1) * 640] for i in range(2)]; bCsb = [Buf("csb0"), Buf("csb1")]
        WG = sb("WG", [128, 4, 512], BF16); bWG = Buf("WG")
        WA = sb("WA", [16, 256], BF16); bWA = Buf("WA")
        BA = sb("BA", [1, 256], BF16); bBA = Buf("BA")
        ones_bf = sb("ones_bf", [128, 128], BF16); bOnes = Buf("ones")
        onesA = sb("onesA", [128, 128], BF16); onesB = sb("onesB", [128, 128], BF16)
        ones_row = sb("ones_row", [1, 128], BF16)
        xr = sb("xr", [128, 8, 128]); bXr = Buf("xr")
        flgs = sb("flgs", [128, 2]); bFlg = Buf("flg")
        bSnd = [Buf("snd%d" % i) for i in range(3)]; bRcv = [Buf("rcv%d" % i) for i in range(3)]
        one11 = sb("one11", [1, 1], F32)
        epsc = sb("epsc", [128, 1]); onec = sb("onec", [128, 1])
        bDram = [Buf("dram%d" % i) for i in range(NB)]
        triu = sb("triu", [128, 128]); trils = sb("trils", [128, 128]); bTri = Buf("tri")
        masku4 = sb("masku4", [128, 4, 128]); bMU = Buf("mu")
        swam = sb("swam", [128, 8, 4, 128], BF16); bSM = Buf("sm")
        iota = sb("iota", [128, 128]); onesz = sb("onesz", [128, 128]); bIo = Buf("iota")
        cTs = sb("cTs", [128, 8]); bcT = Buf("cT")
        modrow = sb("modrow", [1, 256]); bModrow = Buf("modrow")
        bmrow = sb("bmrow", [1, 256]); bBmrow = Buf("bmrow")
        wmst = sb("wmst", [128, 8, 256]); bWmst = Buf("wmst")
        modc = sb("modc", [128, 24]); bModc = Buf("modc")
        gp = sb("gp", [128, 8]); gpo = sb("gpo", [128, 8]); bGp = Buf("gp")
        gs = sb("gs", [128, 8]); gg = sb("gg", [128, 8]); bGs = Buf("gs")
        snk = sb("snk", [128, 8]); esnk = sb("esnk", [128, 8]); bSnk = Buf("snk")
        gglas = sb("gglas", [128, 4]); s5ds = sb("s5ds", [128, 4]); bglus = sb("bglus", [128, 4]); bSm = Buf("small")
        colp = sb("colp", [128, 3, 16]); bColp = Buf("colp")
        colw = sb("colw", [128, 12, 16]); bColw = Buf("colw")
        magc = sb("magc", [128, 16]); thc = sb("thc", [128, 16]); bMag = Buf("mag")
        big = [WOf[:, 2304 + i * 256:2304 + (i + 1) * 256] for i in range(6)]; bBig = [Buf("big%d" % i) for i in range(6)]
        bigi = WOi[:, 5888:6144]; bBigi = Buf("bigi")
        rep = WOf[:, 3840:4608].rearrange("p (a b) -> p a b", a=3); bRep = Buf("rep")
        wrep = [WOf[:, i * 256:(i + 1) * 256] for i in range(9)]; bWrep = Buf("wrep")
        BbT = sb("BbT", [128, 2, 16, 128], BF16); bBbT = Buf("BbT")
        CmT = sb("CmT", [128, 2, 16, 128], BF16); bCmT = Buf("CmT"); bCmT1 = Buf("CmT1")
        crT = sb("crT", [128, 16, 128]); srT = sb("srT", [128, 16, 128]); rzT = sb("rzT", [128, 16, 128]); bTab = Buf("tab")
        carry = sb("carry", [128, 2, 16]); bCarry = Buf("carry")
        ctmp = sb("ctmp", [128, 2, 16]); bCtmp = Buf("ctmp")
        xb = [sb("xb%d" % i, [128, 8, 128]) for i in range(2)]; bXb = [Buf("xb0"), Buf("xb1")]
        sq = sb("sq", [128, 8, 128], BF16); bSq = Buf("sq")
        rstd = sb("rstd", [128, 128]); bRstd = Buf("rstd")
        xn = sb("xn", [128, 8, 128]); bXn = Buf("xn")
        hT = sb("hT", [128, 8, 128], BF16); bHT = Buf("hT")
        mixT = sb("mixT", [128, 16, 128], BF16); bMix = [Buf("mix%d" % i) for i in range(16)]
        ob0 = sb("ob0", [128, 8, 128]); ob = [ob0, ob0]; bOb0 = Buf("ob0"); bOb = [bOb0, bOb0]
        qT = sb("qT", [128, 16, 128], BF16); bQT = Buf("qT")
        kT = [sb("kT%d" % i, [128, 2, 128], BF16) for i in range(2)]; bKT = [Buf("kT0"), Buf("kT1")]
        Vp = [sb("Vp%d" % i, [128, 4, 128], BF16) for i in range(2)]; bVp = [Buf("Vp0"), Buf("Vp1")]
        pexp = sb("pexp", [128, 4, 128]); bPexp = Buf("pexp")
        pT = sb("pT", [128, 4, 128], BF16); bPT = Buf("pT")
        rden = sb("rden", [128, 128]); bRden = Buf("rden")
        onrm = sb("onrm", [128, 128]); bOnrm = Buf("onrm")
        sg = sb("sg", [128, 8, 128]); bSg = Buf("sg")
        lrT = sb("lrT", [16, 128], BF16); bLrT = Buf("lrT")
        lap = sb("lap", [128, 256]); bLap = Buf("lap")
        EqT = sb("EqT", [128, 2, 128]); EkT = sb("EkT", [128, 2, 128]); Er = sb("Er", [128, 256]); bEx = Buf("Ex")
        qtl = sb("qtl", [128, 4, 128], BF16); ktl = sb("ktl", [128, 2, 128], BF16); bQK = Buf("qk")
        kpr = sb("kpr", [128, 256], BF16); vtok = sb("vtok", [128, 512], BF16); bKV = Buf("kv")
        attn = sb("attn", [128, 4, 128], BF16); bAttn = Buf("attn")
        Sst = sb("Sst", [128, 2, 128]); Sbf = sb("Sbf", [128, 2, 128], BF16); bS = Buf("S"); bSb = Buf("Sb")
        gsq = sb("gsq", [128, 4, 128], BF16); bGsq = Buf("gsq")
        grs = sb("grs", [128, 4, 128]); bGrs = Buf("grs")
        gon = sb("gon", [128, 4, 128]); bGon = Buf("gon")
        gsg = sb("gsg", [128, 4, 128]); bGsg = Buf("gsg")
        uTb = sb("uTb", [128, 4, 128], BF16); uTf = sb("uTf", [128, 4, 128]); bU = Buf("u")
        wk = [sb("wk%d" % i, [128, 4, 128]) for i in range(4)]; bWk = [Buf("wk%d" % i) for i in range(4)]
        hb = sb("hb", [128, 4, 4, 128], BF16); bHb = Buf("hb")
        z = sb("z", [128, 4, 128]); z2 = sb("z2", [128, 4, 128]); bZ = Buf("z"); bZ2 = Buf("z2")
        ybf = sb("ybf", [128, 4, 128], BF16); yf = sb("yf", [128, 4, 128]); bY = Buf("y")
        csg = sb("csg", [128, 4, 128]); bCsg = Buf("csg")
        gsig = sb("gsig", [128, 4, 128]); bGsig = Buf("gsig")
        ps = E(nc.psum_tensor("ps", [128, 8, 512], F32))
        bP = [Buf("psum%d" % i) for i in range(8)]

        def mm(out, lhsT, rhs, start, stop, reads, writes, inc=False):
            P.op(pe, lambda e: e.matmul(out, lhsT=lhsT, rhs=rhs, start=start, stop=stop), reads, writes, inc=inc)

        def A(out, in_, func, reads, writes, bias=None, scale=None):
            kw = {}
            if bias is not None:
                kw["bias"] = bias
            if scale is not None:
                kw["scale"] = scale
            P.op(act, lambda e: e.activation(out=out, in_=in_, func=func, **kw), reads, writes)

        def TT(eng, out, in0, in1, op, reads, writes):
            P.op(eng, lambda e: e.tensor_tensor(out=out, in0=in0, in1=in1, op=op), reads, writes)

        def TS(eng, out, in0, s1, s2, op0, op1, reads, writes):
            if op1 is None:
                P.op(eng, lambda e: e.tensor_scalar(out=out, in0=in0, scalar1=s1, scalar2=None, op0=op0), reads, writes)
            else:
                P.op(eng, lambda e: e.tensor_scalar(out=out, in0=in0, scalar1=s1, scalar2=s2, op0=op0, op1=op1), reads, writes)

        def STT(out, in0, scalar, in1, op0, op1, reads, writes):
            P.op(dve, lambda e: e.scalar_tensor_tensor(out=out, in0=in0, scalar=scalar, in1=in1, op0=op0, op1=op1), reads, writes)

        def CP(eng, out, in_, reads, writes):
            if eng is act:
                P.op(eng, lambda e: e.activation(out=out, in_=in_, func=AF.Copy), reads, writes)
            else:
                P.op(eng, lambda e: e.tensor_copy(out=out, in_=in_), reads, writes)

        def MS(eng, ap, val, writes):
            P.op(eng, lambda e: e.memset(ap, val), (), writes)

        def f2(ap):
            return ap.rearrange("p a b -> p (a b)")

        def conv_jobs(l):
            jobs = []
            for r in range(8):
                for p_ in range(8):
                    jobs.append((win[l][r * 128:(r + 1) * 128, p_ * CW:(p_ + 1) * CW],
                                 wbfL[l][p_ * 5:(p_ + 1) * 5, :, r, :].rearrange("t p c -> p t c"), CW, bWdL[l], 5))
                jobs.append((win[l][r * 128:(r + 1) * 128, 5120:5136], wlrL[l][:, r, :], 16, bWdL[l], 0))
            for r in range(16):
                for p_ in range(2):
                    jobs.append((wout[l][r * 128:(r + 1) * 128, p_ * 512:(p_ + 1) * 512],
                                 wobfL[l][p_ * 4:(p_ + 1) * 4, :, r, :].rearrange("o p c -> p o c"), 512, bWOdL[l], 4))
            return jobs

        def conv_in(job):
            i = wctr[2] % 2; wctr[2] += 1
            src, dst, ncol, dbuf, nt = job
            P.dma(sp, cst[i][:, 0:ncol], src, (), [bCst[i]], bCst[i])
            return i

        def conv_out(job, i, eng, q=None):
            q = q or sp
            src, dst, ncol, dbuf, nt = job
            CP(eng, csb[i][:, 0:ncol], cst[i][:, 0:ncol], [bCst[i]], [bCsb[i]])
            srcv = csb[i][:, 0:ncol] if nt == 0 else csb[i][:, 0:ncol].rearrange("p (t c) -> p t c", t=nt)
            P.dma(q, dst, srcv, [bCsb[i]], [dbuf], dbuf)

        def load_cast(dst_ap, src_ap, ncol, eng, dbuf):
            i = wctr[2] % 2; wctr[2] += 1
            P.dma(sp, cst[i][:, 0:ncol], src_ap, (), [bCst[i]], bCst[i])
            CP(eng, dst_ap, cst[i][:, 0:ncol], [bCst[i]], [dbuf])

        P.dma(sp, triu[:], c_triu, (), [bTri], bTri)
        P.dma(sp, trils[:], c_trils, (), [bTri], bTri)
        P.dma(sp, f2(masku4[:]), c_masku4, (), [bMU], bMU)
        for q_ in range(8):
            load_cast(swam[:].rearrange("p a b c -> p (a b c)")[:, q_ * 512:(q_ + 1) * 512], c_swam[:, q_ * 512:(q_ + 1) * 512], 512, (dve, act)[q_ % 2], bSM)
        P.dma(sp, iota[:], c_iota, (), [bIo], bIo)
        P.dma(sp, onesz[:], c_onesz, (), [bIo], bIo)
        P.dma(sp, cTs[:], cT, (), [bcT], bcT)
        P.dma(sp, flgs[:], flg, (), [bFlg], bFlg)
        MS(dve, ones_bf[:], 1.0, [bOnes])
        MS(dve, onesA[:], 0.0, [bOnes]); MS(dve, onesB[:], 0.0, [bOnes])
        MS(dve, onesA[:, 0:64], 1.0, [bOnes]); MS(dve, onesB[:, 64:128], 1.0, [bOnes])
        MS(dve, ones_row[:], 1.0, [bOnes]); MS(dve, one11[:], 1.0, [bOnes])
        MS(dve, epsc[:], EPS, [bOnes]); MS(dve, onec[:], 1.0, [bOnes])
        A(cTs[:], cTs[:], AF.Silu, [bcT], [bcT])

        def range_sin(eng_list, out, ang, tmp, tmpi, n, rb, wb_):
            TS(dve, tmp, ang, 1.0 / TWO_PI, None, ALU.mult, None, rb, wb_)
            CP(dve, tmpi, tmp, wb_, wb_)
            CP(dve, tmp, tmpi, wb_, wb_)
            STT(tmp, tmp, -TWO_PI, ang, ALU.mult, ALU.add, rb + wb_, wb_)
            TS(dve, out, tmp, math.pi, -TWO_PI, ALU.is_gt, ALU.mult, wb_, wb_)
            TT(dve, tmp, tmp, out, ALU.add, wb_, wb_)
            TS(dve, out, tmp, -math.pi, TWO_PI, ALU.is_lt, ALU.mult, wb_, wb_)
            TT(dve, tmp, tmp, out, ALU.add, wb_, wb_)
            A(out, tmp, AF.Sin, wb_, wb_)

        try:
          if STOP == 1:
            raise _Stop()
          for l in range(NL):
            src_x = xT if l == 0 else x1T
            dst_x = outT if l == NL - 1 else x1T
            wbf, wlr, wobf, bWd, bWOd = wbfL[l], wlrL[l], wobfL[l], bWdL[l], bWOdL[l]
            P.dma(sp, gp[:], gpre[l], (), [bGp], bGp)
            P.dma(sp, gpo[:], gpost[l], (), [bGp], bGp)
            P.dma(sp, snk[:], sinks[l], (), [bSnk], bSnk)
            P.dma(sp, gglas[:], ggla[l], (), [bSm], bSm)
            P.dma(sp, s5ds[:], s5d[l], (), [bSm], bSm)
            P.dma(sp, bglus[:], bglu[l], (), [bSm], bSm)
            P.dma(sp, colp[:], s5col[l], (), [bColp], bColp)
            if l == 0:
                jobs0 = conv_jobs(0)
                engs = (pool, pool, pool)
                pend = [conv_in(jobs0[0])]
                for ji in range(len(jobs0)):
                    if ji + 1 < len(jobs0):
                        pend.append(conv_in(jobs0[ji + 1]))
                    conv_out(jobs0[ji], pend[ji], engs[ji % 3], q=pool)
            bgjobs = conv_jobs(l + 1) if l + 1 < NL else []
            bgpend = []

            def bg_service(flush=False):
                while True:
                    for jb, slot in bgpend:
                        conv_out(jb, slot, pool, q=pool)
                    del bgpend[:]
                    for _ in range(2):
                        if bgjobs:
                            jb = bgjobs.pop(0)
                            bgpend.append((jb, conv_in(jb)))
                    if not flush or not bgpend:
                        break
            for k_ in range(4):
                load_cast(WG[:, k_, :], wglu[l][k_ * 128:(k_ + 1) * 128, :], 512, pool, bWG)
            P.dma(pool, WA[:], walpha[l], (), [bWA], bWA)
            P.dma(pool, BA[:], balpha[l], (), [bBA], bBA)
            def abar_calc(are, aim, ldt, w, n, rb, wbuf, tmpi):
                dt_, th, mag, sn, cs, t0, t1, fr, fi = w[:9]
                A(dt_, ldt, AF.Exp, rb, wbuf)
                TT(dve, th, aim, dt_, ALU.mult, rb + wbuf, wbuf)
                TT(dve, t0, are, dt_, ALU.mult, rb + wbuf, wbuf)
                A(mag, t0, AF.Exp, wbuf, wbuf)
                range_sin(None, sn, th, t0, tmpi, n, wbuf, wbuf)
                TS(dve, t1, th, math.pi / 2, None, ALU.add, None, wbuf, wbuf)
                range_sin(None, cs, t1, t0, tmpi, n, wbuf, wbuf)
                TT(dve, cs, cs, mag, ALU.mult, wbuf, wbuf)
                TT(dve, sn, sn, mag, ALU.mult, wbuf, wbuf)
                TT(dve, t0, are, are, ALU.mult, rb + wbuf, wbuf)
                TT(dve, t1, aim, aim, ALU.mult, rb + wbuf, wbuf)
                TT(dve, t0, t0, t1, ALU.add, wbuf, wbuf)
                P.op(dve, lambda e: e.reciprocal(out=t0, in_=t0), wbuf, wbuf)
                TS(dve, t1, cs, -1.0, None, ALU.add, None, wbuf, wbuf)
                TT(dve, fr, t1, are, ALU.mult, rb + wbuf, wbuf)
                TT(dve, dt_, sn, aim, ALU.mult, rb + wbuf, wbuf)
                TT(dve, fr, fr, dt_, ALU.add, wbuf, wbuf)
                TT(dve, fr, fr, t0, ALU.mult, wbuf, wbuf)
                TT(dve, fi, sn, are, ALU.mult, rb + wbuf, wbuf)
                TT(dve, dt_, t1, aim, ALU.mult, rb + wbuf, wbuf)
                TT(dve, fi, fi, dt_, ALU.subtract, wbuf, wbuf)
                TT(dve, fi, fi, t0, ALU.mult, wbuf, wbuf)
                return cs, sn, fr, fi, mag, th

            cw = [colw[:, i, :] for i in range(9)]
            _, _, _, _, magv, thv = abar_calc(colp[:, 0, :], colp[:, 1, :], colp[:, 2, :], cw, 16, [bColp], [bColw], bigi[:, 0:16])
            CP(dve, magc[:], magv, [bColw], [bMag]); CP(dve, thc[:], thv, [bColw], [bMag])
            for q_ in range(4):
                load_cast(CmT[:, 0].rearrange("p a b -> p (a b)")[:, q_ * 512:(q_ + 1) * 512], cpad[l, 0][:, q_ * 512:(q_ + 1) * 512], 512, pool, bCmT)
            for sl in range(8):
                cs_ = slice(sl * 256, (sl + 1) * 256)
                for jj in range(2):
                    j = sl * 2 + jj
                    TS(dve, big[0][:, jj * 128:(jj + 1) * 128], iota[:], thc[:, j:j + 1], None, ALU.mult, None, [bIo, bMag], [bBig[0]])
                    TS(dve, rzT[:, j, :], onesz[:], magc[:, j:j + 1], None, ALU.mult, None, [bIo, bMag], [bTab])
                range_sin(None, f2(srT[:])[:, cs_], big[0][:], big[1][:], bigi[:], 256, [bBig[0]], [bBig[1], bBigi, bTab])
                TS(dve, big[2][:], big[0][:], math.pi / 2, None, ALU.add, None, [bBig[0]], [bBig[2]])
                range_sin(None, f2(crT[:])[:, cs_], big[2][:], big[1][:], bigi[:], 256, [bBig[2]], [bBig[1], bBigi, bTab])
                P.dma(act, rep[:], s5rep[l][:, :, cs_], (), [bRep], bRep)
                _, _, frv, fiv, _, _ = abar_calc(rep[:, 0, :], rep[:, 1, :], rep[:, 2, :], [w_[:] for w_ in wrep], 256, [bRep], [bWrep], bigi[:])
                P.dma(act, big[3][:], btpad[l, 0][:, cs_], (), [bBig[3]], bBig[3])
                P.dma(act, big[4][:], btpad[l, 1][:, cs_], (), [bBig[4]], bBig[4])
                TT(dve, big[0][:], frv, big[3][:], ALU.mult, [bWrep, bBig[3]], [bBig[0]])
                TT(dve, big[1][:], fiv, big[4][:], ALU.mult, [bWrep, bBig[4]], [bBig[1]])
                TT(dve, BbT[:, 0].rearrange("p a b -> p (a b)")[:, cs_], big[0][:], big[1][:], ALU.subtract, [bBig[0], bBig[1]], [bBbT])
                TT(dve, big[0][:], frv, big[4][:], ALU.mult, [bWrep, bBig[4]], [bBig[0]])
                TT(dve, big[1][:], fiv, big[3][:], ALU.mult, [bWrep, bBig[3]], [bBig[1]])
                TT(dve, BbT[:, 1].rearrange("p a b -> p (a b)")[:, cs_], big[0][:], big[1][:], ALU.add, [bBig[0], bBig[1]], [bBbT])
                P.dma(act, big[5][:], cpad[l, 1][:, cs_], (), [bBig[5]], bBig[5])
                TS(dve, CmT[:, 1].rearrange("p a b -> p (a b)")[:, cs_], big[5][:], -1.0, None, ALU.mult, None, [bBig[5]], [bCmT1])
            if STOP == 3:
                raise _Stop()
            for ch in range(12):
                P.dma(act, wmst[:], wmod[l][:, ch * 256:(ch + 1) * 256].rearrange("(k p) c -> p k c", p=128), (), [bWmst], bWmst)
                P.dma(act, bmrow[:], bmod[l][:, ch * 256:(ch + 1) * 256], (), [bBmrow], bBmrow)
                for k in range(8):
                    mm(ps[0:1, 0, 0:256], cTs[:, k:k + 1], wmst[:, k, :], k == 0, False, [bcT, bWmst], [bP[0]])
                mm(ps[0:1, 0, 0:256], one11[:], bmrow[:], False, True, [bOnes, bBmrow], [bP[0]], inc=True)
                CP(act, modrow[:], ps[0:1, 0, 0:256], [bP[0]], [bModrow])
                for tt in range(2):
                    t = ch * 2 + tt
                    mm(ps[:, 1, t:t + 1], modrow[0:1, tt * 128:(tt + 1) * 128], one11[:], True, True, [bModrow, bOnes], [bP[1]], inc=True)
            CP(dve, modc[:], ps[:, 1, 0:24], [bP[1]], [bModc])
            if STOP == 2:
                raise _Stop()
            STT(gs[:], modc[:, 8:16], 1.0, gp[:], ALU.add, ALU.mult, [bModc, bGp], [bGs])
            TT(dve, gg[:], modc[:, 16:24], gpo[:], ALU.mult, [bModc, bGp], [bGs])
            A(esnk[:], snk[:], AF.Exp, [bSnk], [bSnk])
            MS(dve, carry[:], 0.0, [bCarry])
            MS(dve, qT[:], 0.0, [bQT])
            MS(dve, qtl[:], 0.0, [bQK])
            MS(dve, Sst[:], 0.0, [bS]); MS(dve, Sbf[:], 0.0, [bSb])
            for i in range(2):
                MS(dve, kT[i][:], 0.0, [bKT[i]]); MS(dve, Vp[i][:], 0.0, [bVp[i]])

            def prenorm(n):
                cur = n % 2
                tsl = slice(n * 128, (n + 1) * 128)
                xt = xb[cur]
                P.dma(sp, xt[:], src_x[:, tsl].rearrange("(k p) t -> p k t", p=128), [], [bXb[cur]], bXb[cur])
                if n >= LAG:
                    sl = (n - LAG) % 3
                    P.dma(sp, xr[:], rcvL[sl][0:D, :].rearrange("(k p) t -> p k t", p=128), [bRcv[sl]], [bXr], bXr)
                    STT(xt[:].rearrange("p a b -> p (a b)"), xr[:].rearrange("p a b -> p (a b)"), flgs[:, 1:2], xt[:].rearrange("p a b -> p (a b)"),
                        ALU.mult, ALU.add, [bXr, bFlg, bXb[cur]], [bXb[cur]])
                A(sq[:], xt[:], AF.Square, [bXb[cur]], [bSq])
                for k in range(8):
                    mm(ps[:, 2, 256:384], ones_bf[:], sq[:, k, :], k == 0, k == 7, [bOnes, bSq], [bP[2]], inc=(k == 7))
                A(rstd[:], ps[:, 2, 256:384], AF.Ln, [bP[2], bOnes], [bRstd], bias=epsc[:], scale=1.0 / D)
                A(rstd[:], rstd[:], AF.Exp, [bRstd], [bRstd], scale=-0.5)
                for k in range(8):
                    TT(dve, xn[:, k, :], xt[:, k, :], rstd[:], ALU.mult, [bXb[cur], bRstd], [bXn])
                    TS(dve, hT[:, k, :], xn[:, k, :], gs[:, k:k + 1], modc[:, k:k + 1], ALU.mult, ALU.add, [bXn, bGs, bModc], [bHT])

            scratchB = [bWrep, bRep, bBigi] + bBig + bCst + bCsb
            for o in range(8):
                P.dma(sp, WO4[:, o], wobf[o], [bWOd], [bWO] + scratchB, bWO)
            prenorm(0)
            for n in range(NB):
                cur = n % 2
                prv = 1 - cur
                tsl = slice(n * 128, (n + 1) * 128)
                xt = xb[cur]
                if n == LAG:
                    fa = flgs[:, 0:1]
                    TS(dve, carry[:].rearrange("p a b -> p (a b)"), carry[:].rearrange("p a b -> p (a b)"), fa, None, ALU.mult, None, [bCarry, bFlg], [bCarry])
                    TS(dve, f2(Sst[:]), f2(Sst[:]), fa, None, ALU.mult, None, [bS, bFlg], [bS])
                    CP(act, Sbf[:], Sst[:], [bS], [bSb])
                    TS(dve, f2(kT[prv][:]), f2(kT[prv][:]), fa, None, ALU.mult, None, [bKT[prv], bFlg], [bKT[prv]])
                    TS(dve, f2(Vp[prv][:]), f2(Vp[prv][:]), fa, None, ALU.mult, None, [bVp[prv], bFlg], [bVp[prv]])

                def wslot(c0, ncols):
                    if ncols <= 128:
                        i = wctr[0] % NWS; wctr[0] += 1
                        tile_, tb = Wr[i], bWr[i]
                        if ncols == 128:
                            P.dma(sp, tile_[:], wbf[c0 // 128], [bWd], [tb], tb)
                        else:
                            P.dma(sp, tile_[:, :, 0:ncols], wlr, [bWd], [tb], tb)
                        return (lambda k: tile_[:, k, 0:ncols]), tb
                    nt = ncols // 128
                    t0 = c0 // 128
                    P.dma(sp, Wbig[:, 0:nt].rearrange("p t k c -> p t (k c)"),
                          wbf[t0:t0 + nt].rearrange("t p k c -> p t (k c)"), [bWd], [bWbig], bWbig)
                    return (lambda k: Wbig[:, 0:nt, k, :]), bWbig

                def proj(pout, c0, ncols, pbuf, inc_last=True):
                    wt, tb = wslot(c0, ncols)
                    for k in range(8):
                        mm(pout, wt(k), hT[:, k, :], k == 0, k == 7, [tb, bHT], [pbuf], inc=(k == 7 and inc_last))

                def proj_tok(pout, c0, ncols, pbuf):
                    wt, tb = wslot(c0, ncols)
                    for k in range(8):
                        mm(pout, hT[:, k, :], wt(k), k == 0, k == 7, [tb, bHT], [pbuf], inc=(k == 7))

                def gen_swa():
                    for half in range(2):
                        for j in range(4):
                            proj(ps[:, half, j * 128:(j + 1) * 128], O_AQ + (half * 4 + j) * 128, 128, bP[half])
                        for jq in range(4):
                            jt = half * 4 + jq
                            CP(act, qT[0:64, 2 * jt, :], ps[0:64, half, jq * 128:(jq + 1) * 128], [bP[half]], [bQT])
                            CP(act, qT[64:128, 2 * jt + 1, :], ps[64:128, half, jq * 128:(jq + 1) * 128], [bP[half]], [bQT])
                        yield
                    for j in range(2):
                        proj(ps[:, 2, j * 128:(j + 1) * 128], O_AK + j * 128, 128, bP[2])
                    CP(act, kT[cur][:], ps[:, 2, 0:256].rearrange("p (a b) -> p a b", a=2), [bP[2]], [bKT[cur]])
                    yield
                    proj_tok(ps[:, 2, 256:512], O_AV, 256, bP[2])
                    for g in range(4):
                        c0 = 0 if g % 2 == 0 else 64
                        CP(dve, Vp[cur][:, g, c0:c0 + 64], ps[:, 2, 256 + g * 64:256 + (g + 1) * 64], [bP[2]], [bVp[cur]])
                    yield
                    for half in range(2):
                        for j in range(4):
                            proj(ps[:, half, j * 128:(j + 1) * 128], O_AG + (half * 4 + j) * 128, 128, bP[half])
                        A(sg[:, half * 4:(half + 1) * 4, :], ps[:, half, :].rearrange("p (a b) -> p a b", a=4), AF.Silu, [bP[half]], [bSg])
                        yield
                    nq = 4 if n > 0 else 2

                    def scores(j):
                        kt = 0 if j < 4 else 1
                        pb = j % 2
                        for idx in range(nq):
                            ksrc = kT[cur] if idx < 2 else kT[prv]
                            kb = bKT[cur] if idx < 2 else bKT[prv]
                            mm(ps[:, pb, idx * 128:(idx + 1) * 128], ksrc[:, kt, :], qT[:, 2 * j + (idx % 2), :], True, True,
                               [kb, bQT], [bP[pb]], inc=(idx == nq - 1))
                    scores(0)
                    for j in range(8):
                        gA, gB = (0, 1) if j < 4 else (2, 3)
                        pb = j % 2
                        A(f2(pexp[:])[:, 0:nq * 128], ps[:, pb, 0:nq * 128], AF.Exp, [bP[pb]], [bPexp], scale=0.125)
                        TT(dve, f2(pT[:])[:, 0:nq * 128], f2(pexp[:])[:, 0:nq * 128], swam[:, j].rearrange("p a b -> p (a b)")[:, 0:nq * 128],
                           ALU.mult, [bPexp, bSM], [bPT])
                        if n == LAG:
                            TS(dve, f2(pT[:])[:, 256:512], f2(pT[:])[:, 256:512], flgs[:, 0:1], None, ALU.mult, None, [bPT, bFlg], [bPT])
                        if j + 1 < 8:
                            scores(j + 1)
                        for idx in range(nq):
                            g = (gA if idx % 2 == 0 else gB)
                            vsrc = Vp[cur] if idx < 2 else Vp[prv]
                            vb = bVp[cur] if idx < 2 else bVp[prv]
                            mm(ps[:, 2, 0:128], vsrc[:, g, :], pT[:, idx, :], idx == 0, idx == nq - 1, [vb, bPT], [bP[2]])
                        for idx in range(nq):
                            mm(ps[:, 2, 128:256], (onesA if idx % 2 == 0 else onesB)[:], pT[:, idx, :], idx == 0, idx == nq - 1,
                               [bOnes, bPT], [bP[2]], inc=(idx == nq - 1))
                        A(rden[:], ps[:, 2, 128:256], AF.Ln, [bP[2], bSnk], [bRden], bias=esnk[:, j:j + 1], scale=1.0)
                        A(rden[:], rden[:], AF.Exp, [bRden], [bRden], scale=-1.0)
                        TT(dve, onrm[:], ps[:, 2, 0:128], rden[:], ALU.mult, [bP[2], bRden], [bOnrm])
                        TT(dve, mixT[:, j, :], onrm[:], sg[:, j, :], ALU.mult, [bOnrm, bSg], [bMix[j]])
                        yield

                def gen_gla():
                    proj(ps[0:16, 3, 0:128], O_LR, 16, bP[3])
                    CP(act, lrT[:], ps[0:16, 3, 0:128], [bP[3]], [bLrT])
                    mm(ps[:, 4, 0:256], lrT[:], WA[:], True, False, [bLrT, bWA], [bP[4]])
                    mm(ps[:, 4, 0:256], ones_row[:], BA[:], False, True, [bOnes, bBA], [bP[4]], inc=True)
                    A(lap[:], ps[:, 4, 0:256], AF.Exp, [bP[4]], [bLap], scale=-1.0)
                    A(lap[:], lap[:], AF.Ln, [bLap, bOnes], [bLap], bias=onec[:], scale=1.0)
                    yield
                    for t in range(2):
                        mm(ps[:, 3, t * 128:(t + 1) * 128], lap[:, t * 128:(t + 1) * 128], triu[:], True, True, [bLap, bTri], [bP[3]], inc=(t == 1))
                    mm(ps[:, 4, 256:512], trils[:], lap[:], True, True, [bLap, bTri], [bP[4]], inc=True)
                    A(f2(EqT[:]), ps[:, 3, 0:256], AF.Exp, [bP[3]], [bEx], scale=-1.0 / 16)
                    A(f2(EkT[:]), ps[:, 3, 0:256], AF.Exp, [bP[3]], [bEx], scale=1.0 / 16)
                    A(Er[:], ps[:, 4, 256:512], AF.Exp, [bP[4]], [bEx], scale=-1.0 / 16)
                    yield
                    for t in range(2):
                        proj(ps[:, 3, t * 128:(t + 1) * 128], O_BQ + t * 128, 128, bP[3])
                    for t in range(2):
                        proj(ps[:, 3, 256 + t * 128:256 + (t + 1) * 128], O_BK + t * 128, 128, bP[3])
                    for h in range(4):
                        t, r0 = h // 2, (h % 2) * 64
                        STT(qtl[r0:r0 + 64, h, :], ps[r0:r0 + 64, 3, t * 128:(t + 1) * 128], 0.125, EqT[r0:r0 + 64, t, :], ALU.mult, ALU.mult, [bP[3], bEx], [bQK])
                    TT(dve, f2(ktl[:]), ps[:, 3, 256:512], f2(EkT[:]), ALU.mult, [bP[3], bEx], [bQK])
                    yield
                    proj_tok(ps[:, 4, 0:256], O_BK, 256, bP[4])
                    TT(dve, kpr[:], ps[:, 4, 0:256], Er[:], ALU.mult, [bP[4], bEx], [bKV])
                    yield
                    proj_tok(ps[:, 3, :], O_BV, 512, bP[3])
                    CP(act, vtok[:], ps[:, 3, :], [bP[3]], [bKV])
                    yield
                    for h in range(4):
                        t = h // 2
                        mm(ps[:, 4, h * 128:(h + 1) * 128], ktl[:, t, :], qtl[:, h, :], True, True, [bQK], [bP[4]], inc=(h == 3))
                    TT(dve, f2(attn[:]), ps[:, 4, :], f2(masku4[:]), ALU.mult, [bP[4], bMU], [bAttn])
                    yield
                    for h in range(4):
                        t = h // 2
                        mm(ps[:, 3, h * 128:(h + 1) * 128], vtok[:, h * 128:(h + 1) * 128], attn[:, h, :], True, False, [bKV, bAttn], [bP[3]])
                        mm(ps[:, 3, h * 128:(h + 1) * 128], Sbf[:, t, :], qtl[:, h, :], False, True, [bSb, bQK], [bP[3]], inc=(h == 3))
                    for h in range(4):
                        t = h // 2
                        mm(ps[:, 4, h * 128:(h + 1) * 128], kpr[:, t * 128:(t + 1) * 128], vtok[:, h * 128:(h + 1) * 128], True, True, [bKV], [bP[4]], inc=(h == 3))
                    for h in range(4):
                        t, r0 = h // 2, (h % 2) * 64
                        STT(Sst[r0:r0 + 64, t, :], Sst[r0:r0 + 64, t, :], EqT[r0:r0 + 64, t, 127:128], ps[r0:r0 + 64, 4, h * 128:(h + 1) * 128],
                            ALU.mult, ALU.add, [bS, bEx, bP[4]], [bS])
                    CP(act, Sbf[:], Sst[:], [bS], [bSb])
                    yield
                    A(f2(gsq[:]), ps[:, 3, :], AF.Square, [bP[3]], [bGsq])
                    for h in range(4):
                        mm(ps[:, 4, h * 128:(h + 1) * 128], ones_bf[:], gsq[:, h, :], True, True, [bOnes, bGsq], [bP[4]], inc=(h == 3))
                    A(f2(grs[:]), ps[:, 4, :], AF.Ln, [bP[4], bOnes], [bGrs], bias=epsc[:], scale=1.0 / 128)
                    A(f2(grs[:]), f2(grs[:]), AF.Exp, [bGrs], [bGrs], scale=-0.5)
                    TT(dve, f2(gon[:]), ps[:, 3, :], f2(grs[:]), ALU.mult, [bP[3], bGrs], [bGon])
                    yield
                    for h in range(4):
                        proj(ps[:, 4, h * 128:(h + 1) * 128], O_BG + h * 128, 128, bP[4])
                    A(f2(gsg[:]), ps[:, 4, :], AF.Silu, [bP[4]], [bGsg])
                    for h in range(4):
                        STT(mixT[:, 8 + h, :], gon[:, h, :], gglas[:, h:h + 1], gsg[:, h, :], ALU.mult, ALU.mult, [bGon, bSm, bGsg], [bMix[8 + h]])
                    yield

                def gen_s5():
                    for c in range(4):
                        proj(ps[:, 5, c * 128:(c + 1) * 128], O_CU + c * 128, 128, bP[5])
                    CP(act, f2(uTb[:]), ps[:, 5, :], [bP[5]], [bU])
                    CP(act, f2(uTf[:]), ps[:, 5, :], [bP[5]], [bU])
                    yield
                    for c in range(4):
                        for jj in range(4):
                            j = c * 4 + jj
                            for pl in range(2):
                                mm(ps[:, 6 + pl, jj * 128:(jj + 1) * 128], BbT[:, pl, j, :], uTb[:, c, :], True, True,
                                   [bBbT, bU], [bP[6 + pl]], inc=(jj == 3))
                        xr = ps[:, 6, :]
                        xi = ps[:, 7, :]
                        cr = crT[:, c * 4:(c + 1) * 4, :].rearrange("p a b -> p (a b)")
                        sr = srT[:, c * 4:(c + 1) * 4, :].rearrange("p a b -> p (a b)")
                        rz = rzT[:, c * 4:(c + 1) * 4, :].rearrange("p a b -> p (a b)")
                        a_, b_, c_, d_ = [f2(w_[:]) for w_ in wk]
                        bA, bB, bC, bD = bWk
                        rX = [bP[6], bP[7]]
                        TT(dve, a_, xr, cr, ALU.mult, rX + [bTab], [bA])
                        TT(dve, b_, xi, sr, ALU.mult, rX + [bTab], [bB])
                        TT(pool, a_, a_, b_, ALU.add, [bA, bB], [bA])
                        TT(dve, b_, xi, cr, ALU.mult, rX + [bTab], [bB])
                        TT(dve, c_, xr, sr, ALU.mult, rX + [bTab], [bC])
                        TT(pool, b_, b_, c_, ALU.subtract, [bB, bC], [bB])
                        yield
                        for pl, (tile_, tb) in enumerate(((wk[0], bA), (wk[1], bB))):
                            TT(dve, ctmp[:, pl, c * 4:(c + 1) * 4], carry[:, pl, c * 4:(c + 1) * 4], magc[:, c * 4:(c + 1) * 4], ALU.mult,
                               [bCarry, bMag], [bCtmp])
                            TT(dve, tile_[:, :, 0], tile_[:, :, 0], ctmp[:, pl, c * 4:(c + 1) * 4], ALU.add, [tb, bCtmp], [tb])
                        P.op(dve, lambda e: e.tensor_tensor_scan(out=c_, data0=rz, data1=a_, initial=0.0, op0=ALU.mult, op1=ALU.add), [bTab, bA], [bC])
                        P.op(dve, lambda e: e.tensor_tensor_scan(out=d_, data0=rz, data1=b_, initial=0.0, op0=ALU.mult, op1=ALU.add), [bTab, bB], [bD])
                        yield
                        hp = [hb[:, q_].rearrange("p a b -> p (a b)") for q_ in range(4)]
                        TT(dve, hp[0], c_, cr, ALU.mult, [bC, bTab], [bHb])
                        TT(pool, hp[2], c_, sr, ALU.mult, [bC, bTab], [bHb])
                        STT(hp[1], d_, -1.0, sr, ALU.mult, ALU.mult, [bD, bTab], [bHb])
                        TT(pool, hp[3], d_, cr, ALU.mult, [bD, bTab], [bHb])
                        c4 = slice(c * 4, (c + 1) * 4)
                        grl = wk[2][:, :, 127]; gil = wk[3][:, :, 127]
                        crl = crT[:, c4, 127]; srl = srT[:, c4, 127]
                        TT(dve, ctmp[:, 0, c4], grl, crl, ALU.mult, [bC, bTab], [bCtmp])
                        TT(dve, ctmp[:, 1, c4], gil, srl, ALU.mult, [bD, bTab], [bCtmp])
                        TT(dve, carry[:, 0, c4], ctmp[:, 0, c4], ctmp[:, 1, c4], ALU.subtract, [bCtmp], [bCarry])
                        TT(dve, ctmp[:, 0, c4], grl, srl, ALU.mult, [bC, bTab], [bCtmp])
                        TT(dve, ctmp[:, 1, c4], gil, crl, ALU.mult, [bD, bTab], [bCtmp])
                        TT(dve, carry[:, 1, c4], ctmp[:, 0, c4], ctmp[:, 1, c4], ALU.add, [bCtmp], [bCarry])
                        yield
                        first = True
                        for jj in range(4):
                            j = c * 4 + jj
                            for q_ in range(4):
                                last = (jj == 3 and q_ == 3)
                                mm(ps[:, 5, c * 128:(c + 1) * 128], CmT[:, q_ // 2, j, :], hb[:, q_, jj, :], first, last, [bCmT, bCmT1, bHb], [bP[5]], inc=last)
                                first = False
                    for c in range(4):
                        STT(z[:, c, :], uTf[:, c, :], s5ds[:, c:c + 1], ps[:, 5, c * 128:(c + 1) * 128], ALU.mult, ALU.add, [bU, bSm, bP[5]], [bZ])
                    TT(dve, f2(z2[:]), f2(z[:]), f2(z[:]), ALU.mult, [bZ], [bZ2])
                    TS(dve, f2(z2[:]), f2(z2[:]), 0.044715, 1.0, ALU.mult, ALU.add, [bZ2], [bZ2])
                    TT(dve, f2(z2[:]), f2(z2[:]), f2(z[:]), ALU.mult, [bZ, bZ2], [bZ2])
                    A(f2(z2[:]), f2(z2[:]), AF.Sigmoid, [bZ2], [bZ2], scale=1.5957691216057308)
                    TT(dve, f2(yf[:]), f2(z[:]), f2(z2[:]), ALU.mult, [bZ, bZ2], [bY])
                    CP(act, f2(ybf[:]), f2(yf[:]), [bY], [bY])
                    yield
                    for co in range(4):
                        for k in range(4):
                            mm(ps[:, 6, co * 128:(co + 1) * 128], WG[:, k, co * 128:(co + 1) * 128], ybf[:, k, :], k == 0, k == 3, [bWG, bY], [bP[6]],
                               inc=(k == 3 and co == 3))
                    for co in range(4):
                        A(gsig[:, co, :], ps[:, 6, co * 128:(co + 1) * 128], AF.Sigmoid, [bP[6], bSm], [bGsig], bias=bglus[:, co:co + 1], scale=1.0)
                    yield
                    for c in range(4):
                        proj(ps[:, 7, c * 128:(c + 1) * 128], O_CG + c * 128, 128, bP[7])
                    A(f2(csg[:]), ps[:, 7, :], AF.Silu, [bP[7]], [bCsg])
                    TT(dve, f2(z[:]), f2(yf[:]), f2(gsig[:]), ALU.mult, [bY, bGsig], [bZ])
                    TT(dve, mixT[:, 12:16, :].rearrange("p a b -> p (a b)"), f2(z[:]), f2(csg[:]), ALU.mult, [bZ, bCsg], bMix[12:16])
                    yield

                gens = [gen_s5(), gen_swa(), gen_gla()] if INTERLEAVE else []
                if not INTERLEAVE:
                    for g_ in (gen_swa(), gen_gla(), gen_s5()):
                        for _ in g_:
                            pass
                while gens:
                    for g_ in list(gens):
                        try:
                            next(g_)
                        except StopIteration:
                            gens.remove(g_)

                if n + 1 < NB:
                    prenorm(n + 1)
                bg_service()
                if STOP == 7:
                    raise _Stop()
                for o in range(8):
                    bank = o // 4
                    for kt in range(16):
                        mm(ps[:, bank, (o % 4) * 128:(o % 4 + 1) * 128], WO4[:, o, kt, :], mixT[:, kt, :], kt == 0, kt == 15,
                           [bWO, bMix[kt]], [bP[bank]], inc=(kt == 15))
                yps = ps[:, 0:2, :].rearrange("p a b -> p (a b)")
                A(f2(sq[:]), yps, AF.Square, [bP[0], bP[1]], [bSq])
                for k in range(8):
                    mm(ps[:, 2, 0:128], ones_bf[:], sq[:, k, :], k == 0, k == 7, [bOnes, bSq], [bP[2]], inc=(k == 7))
                A(rstd[:], ps[:, 2, 0:128], AF.Ln, [bP[2], bOnes], [bRstd], bias=epsc[:], scale=1.0 / D)
                A(rstd[:], rstd[:], AF.Exp, [bRstd], [bRstd], scale=-0.5)
                for k in range(8):
                    TT(dve, xn[:, k, :], ps[:, k // 4, (k % 4) * 128:(k % 4 + 1) * 128], rstd[:], ALU.mult, [bP[k // 4], bRstd], [bXn])
                    STT(ob[cur][:, k, :], xn[:, k, :], gg[:, k:k + 1], xt[:, k, :], ALU.mult, ALU.add, [bXn, bGs, bXb[cur]], [bOb[cur]])
                sl = n % 3
                P.dma(pool, sndL[sl].rearrange("(k p) t -> p k t", p=128), ob[cur][:], [bOb[cur]], [bSnd[sl]], bSnd[sl])
                P.dma(sp, dst_x[:, tsl].rearrange("(k p) t -> p k t", p=128), ob[cur][:], [bOb[cur]], [bDram[n]], bOb[cur])
                _coll(P, pool, groups, rcvL[sl], sndL[sl], [bSnd[sl]], [bRcv[sl]], bRcv[sl])
        except _Stop:
            pass
        if bOb0.dsem is not None:
            pool.eng.wait_ge(bOb0.dsem, bOb0.dcount)
        for b_ in bSnd + bRcv:
            if b_.dsem is not None:
                pool.eng.wait_ge(b_.dsem, b_.dcount)
    return nc


def _alibi_slopes(n):
    return np.exp2(-8.0 * np.arange(1, n + 1, dtype=np.float64) / n)


def host_prep(inputs, b, T, NL, l0=0, role=0):
    f = lambda a: np.ascontiguousarray(a, dtype=np.float32)
    inputs = dict(inputs)
    for k_ in list(inputs):
        if k_ not in ("x", "c"):
            inputs[k_] = inputs[k_][l0:l0 + NL]
    m = {}
    xt_ = np.zeros((D, T + LAG * 128), np.float32)
    if role == 0:
        xt_[:, :T] = inputs["x"][b, :T].T
    m["xT"] = xt_
    fl = np.zeros((128, 2), np.float32)
    fl[:, role] = 1.0
    m["flg"] = fl
    m["cT"] = f(inputs["c"][b].reshape(8, 128).T)
    m["wmod"] = f(inputs["w_mod"][:NL])
    m["bmod"] = f(inputs["b_mod"][:NL].reshape(NL, 1, 3 * D))
    col8 = lambda a: f(a[:NL].reshape(NL, 8, 128).transpose(0, 2, 1))
    m["gpre"] = col8(inputs["g_pre"])
    m["gpost"] = col8(inputs["g_post"])
    w_in = inputs["w_in"][:NL]
    offs = np.cumsum([0, 1024, 256, 256, 1024, 256, 256, 512, 16, 512, 512, 512])
    aq, ak, av, ag, bq, bk, bv, blr, bg, cu, cg = [w_in[:, :, offs[i]:offs[i + 1]] for i in range(11)]
    hperm = []
    for j in range(8):
        hA, hB = swa_tile_heads(j)
        hperm += list(range(hA * 64, hA * 64 + 64)) + list(range(hB * 64, hB * 64 + 64))
    hperm = np.array(hperm)
    m["win"] = f(np.concatenate([aq[:, :, hperm], ak, ag[:, :, hperm], bq, bk, bg, cu, cg, av, bv, blr], axis=2))
    wo = inputs["w_out"][:NL]
    m["wout"] = f(np.concatenate([wo[:, hperm, :], wo[:, 1024:, :]], axis=1))
    sk = inputs["attn_sinks"][:NL]
    sc = np.zeros((NL, 128, 8), np.float32)
    for j in range(8):
        hA, hB = swa_tile_heads(j)
        sc[:, 0:64, j] = sk[:, hA][:, None]
        sc[:, 64:128, j] = sk[:, hB][:, None]
    m["sinks"] = sc
    m["walpha"] = f(inputs["gla_w_alpha"][:NL])
    m["balpha"] = f(inputs["gla_b_alpha"][:NL].reshape(NL, 1, 256))
    col4 = lambda a: f(a[:NL].reshape(NL, 4, 128).transpose(0, 2, 1))
    m["ggla"] = col4(inputs["gla_norm_g"])
    are, aim, ldt = inputs["s5_a_re"][:NL], inputs["s5_a_im"][:NL], inputs["s5_log_dt"][:NL]
    ldt_e = np.broadcast_to(ldt[:, :, None], are.shape)
    tocol = lambda a: a.reshape(NL, 16, 2, 64).transpose(0, 2, 3, 1).reshape(NL, 128, 16)
    m["s5col"] = f(np.stack([tocol(are), tocol(aim), tocol(ldt_e)], axis=2))
    torep = lambda a: np.broadcast_to(a.reshape(NL, 1, 2048), (NL, 128, 2048))
    m["s5rep"] = f(np.stack([torep(are), torep(aim), torep(ldt_e)], axis=2))
    bt = np.zeros((NL, 2, 128, 16, 128), np.float32)
    cp = np.zeros((NL, 2, 128, 16, 128), np.float32)
    for pl, (bsrc, csrc) in enumerate(((inputs["s5_b_re"], inputs["s5_c_re"]), (inputs["s5_b_im"], inputs["s5_c_im"]))):
        for j in range(16):
            ip = j % 4
            for g2 in range(2):
                g = 2 * j + g2
                r0 = ip * 32 + g2 * 16
                bt[:, pl, r0:r0 + 16, j, g2 * 64:(g2 + 1) * 64] = bsrc[:NL, g].transpose(0, 2, 1)
                cp[:, pl, g2 * 64:(g2 + 1) * 64, j, r0:r0 + 16] = csrc[:NL, g].transpose(0, 2, 1)
    m["btpad"] = bt.reshape(NL, 2, 128, 2048)
    m["cpad"] = cp.reshape(NL, 2, 128, 2048)
    m["s5d"] = col4(inputs["s5_d"])
    m["wglu"] = f(inputs["s5_w_glu"][:NL])
    m["bglu"] = col4(inputs["s5_b_glu"])
    ii = np.arange(128)
    m["c_triu"] = (ii[:, None] <= ii[None, :]).astype(np.float32)
    m["c_trils"] = (ii[:, None] > ii[None, :]).astype(np.float32)
    m["c_masku4"] = np.tile(m["c_triu"], (1, 4)).astype(np.float32)
    slopes = _alibi_slopes(16)
    sm = np.zeros((128, 8, 4, 128), np.float64)
    s_ = ii[:, None].astype(np.float64); t_ = ii[None, :].astype(np.float64)
    for j in range(8):
        hA, hB = swa_tile_heads(j)
        for idx in range(4):
            h = hA if idx % 2 == 0 else hB
            if idx < 2:
                dist = t_ - s_
                valid = dist >= 0
            else:
                dist = t_ - s_ + 128
                valid = dist < 128
            sm[:, j, idx, :] = np.where(valid, np.exp(-slopes[h] * dist), 0.0)
    m["c_swam"] = sm.reshape(128, -1).astype(np.float32)
    m["c_iota"] = np.broadcast_to((ii + 1).astype(np.float32)[None, :], (128, 128)).copy()
    oz = np.ones((128, 128), np.float32); oz[:, 0] = 0.0
    m["c_onesz"] = oz
    return m


_CACHE = {}


def kernel(**inputs):
    T = SEQ
    if T not in _CACHE:
        _CACHE[T] = build(T, 8)
    nc = _CACHE[T]
    inputs = {k: np.asarray(v) for k, v in inputs.items()}
    in_maps = [host_prep(inputs, i // 2, T, 1, l0=i % 2, role=i % 2) for i in range(8)]
    res = run_bass_kernel_spmd(nc, in_maps, core_ids=list(range(8)))
    out = np.stack([res.results[2 * b + 1]["outT"][:, LAG * 128:].T for b in range(4)], axis=0)
    return np.ascontiguousarray(out, dtype=np.float32)
```

```python
import math
from contextlib import ExitStack
import numpy as np
import concourse.bass as bass
import concourse.mybir as mybir
from concourse.bass_utils import run_bass_kernel_spmd

F32 = mybir.dt.float32
BF16 = mybir.dt.bfloat16
I32 = mybir.dt.int32
ALU = mybir.AluOpType
AF = mybir.ActivationFunctionType

D = 1024
SEQ = 4096
NCOL = 5136
O_AQ, O_AK, O_AG, O_BQ, O_BK, O_BG, O_CU, O_CG, O_AV, O_BV, O_LR = (
    0, 1024, 1280, 2304, 2560, 2816, 3328, 3840, 4352, 4608, 5120)
EPS = 1e-6
TWO_PI = 2.0 * math.pi


def swa_tile_heads(j):
    return (j, 4 + j) if j < 4 else (8 + (j - 4), 12 + (j - 4))


class Buf:
    __slots__ = ("w", "rs", "dsem", "dcount", "name")

    def __init__(self, name=""):
        self.w = None
        self.rs = []
        self.dsem = None
        self.dcount = 0
        self.name = name


class Eng:
    def __init__(self, eng, sem, is_pe=False):
        self.eng = eng
        self.sem = sem
        self.count = 0
        self.known = {}
        self.is_pe = is_pe
        self.pending = False


class Prog:
    def __init__(self, nc, es):
        self.nc = nc
        self.es = es
        E = es.enter_context
        self.pe = Eng(nc.tensor, E(nc.semaphore("s_pe")), True)
        self.act = Eng(nc.scalar, E(nc.semaphore("s_act")))
        self.dve = Eng(nc.vector, E(nc.semaphore("s_dve")))
        self.pool = Eng(nc.gpsimd, E(nc.semaphore("s_pool")))
        self.sp = Eng(nc.sync, E(nc.semaphore("s_sp")))
        self.nsem = 0

    def _wait(self, e, deps):
        for (sem, val, owner) in deps:
            if owner is e:
                if e.is_pe or owner is self.sp:
                    continue
                if val < e.count:
                    continue
            k = id(sem)
            if e.known.get(k, 0) >= val:
                continue
            e.eng.wait_ge(sem, val)
            e.known[k] = val

    def _deps(self, reads, writes):
        deps = []
        for b in reads:
            if b.w is not None:
                deps.append(b.w)
        for b in writes:
            if b.w is not None:
                deps.append(b.w)
            deps.extend(b.rs)
        return deps

    def op(self, e, fn, reads=(), writes=(), inc=True):
        self._wait(e, self._deps(reads, writes))
        ins = fn(e.eng)
        tag_val = e.count + 1
        if inc:
            ins.then_inc(e.sem, 1)
            e.count += 1
        tag = (e.sem, tag_val, e)
        for b in writes:
            b.w = tag
            b.rs = []
        for b in reads:
            b.rs.append(tag)
            if len(b.rs) > 24:
                b.rs = b.rs[-24:]
        return ins

    def dma(self, q, out_ap, in_ap, reads=(), writes=(), tagbuf=None):
        self._wait(q, self._deps(reads, writes))
        tb = tagbuf
        if tb.dsem is None:
            tb.dsem = self.es.enter_context(self.nc.semaphore("d%d" % self.nsem))
            self.nsem += 1
        q.eng.dma_start(out=out_ap, in_=in_ap).then_inc(tb.dsem, 16)
        tb.dcount += 16
        tag = (tb.dsem, tb.dcount, None)
        for b in writes:
            b.w = tag
            b.rs = []
        for b in reads:
            b.rs.append(tag)


def _coll(P, q, groups, out_ap, in_ap, reads, writes, tagbuf):
    P._wait(q, P._deps(reads, writes))
    tb = tagbuf
    if tb.dsem is None:
        tb.dsem = P.es.enter_context(P.nc.semaphore("c%d" % P.nsem))
        P.nsem += 1
    q.eng.collective_compute("AllGather", ALU.bypass, replica_groups=groups, ins=[in_ap], outs=[out_ap]).then_inc(tb.dsem, 1)
    tb.dcount += 1
    tag = (tb.dsem, tb.dcount, None)
    for b in writes:
        b.w = tag
        b.rs = []
    for b in reads:
        b.rs.append(tag)


class _Stop(Exception):
    pass


STOP = 0
INTERLEAVE = True


LAG = 2


def build(T, NCORES=8, dbg=False):
    NL = 1
    NB = T // 128 + LAG
    TT_ = NB * 128
    groups = [[2 * i, 2 * i + 1] for i in range(NCORES // 2)]
    nc = bass.Bass("TRN2", target_bir_lowering=False)
    dr = {}

    def din(name, shape):
        dr[name] = nc.dram_tensor(name, list(shape), F32, kind="ExternalInput").ap()
        return dr[name]

    xT = din("xT", [D, TT_])
    flg = din("flg", [128, 2])
    cT = din("cT", [128, 8])
    wmod = din("wmod", [NL, D, 3 * D])
    bmod = din("bmod", [NL, 1, 3 * D])
    gpre = din("gpre", [NL, 128, 8])
    gpost = din("gpost", [NL, 128, 8])
    win = din("win", [NL, D, NCOL])
    wout = din("wout", [NL, 2 * D, D])
    sinks = din("sinks", [NL, 128, 8])
    walpha = din("walpha", [NL, 16, 256])
    balpha = din("balpha", [NL, 1, 256])
    ggla = din("ggla", [NL, 128, 4])
    s5col = din("s5col", [NL, 128, 3, 16])
    s5rep = din("s5rep", [NL, 128, 3, 2048])
    btpad = din("btpad", [NL, 2, 128, 2048])
    cpad = din("cpad", [NL, 2, 128, 2048])
    s5d = din("s5d", [NL, 128, 4])
    wglu = din("wglu", [NL, 512, 512])
    bglu = din("bglu", [NL, 128, 4])
    c_triu = din("c_triu", [128, 128])
    c_trils = din("c_trils", [128, 128])
    c_masku4 = din("c_masku4", [128, 512])
    c_swam = din("c_swam", [128, 8 * 4 * 128])
    c_iota = din("c_iota", [128, 128])
    c_onesz = din("c_onesz", [128, 128])
    outT = nc.dram_tensor("outT", [D, TT_], F32, kind="ExternalOutput").ap()
    x1T = None
    sndL = [nc.dram_tensor("snd%d" % i, [D, 128], F32, kind="Internal", addr_space="Local").ap() for i in range(3)]
    rcvL = [nc.dram_tensor("rcv%d" % i, [2 * D, 128], F32, kind="Internal", addr_space="Local").ap() for i in range(3)]
    wbfL = [nc.dram_tensor("wbf%d" % i, [40, 128, 8, 128], BF16, kind="Internal").ap() for i in range(NL)]
    wlrL = [nc.dram_tensor("wlr%d" % i, [128, 8, 16], BF16, kind="Internal").ap() for i in range(NL)]
    wobfL = [nc.dram_tensor("wobf%d" % i, [8, 128, 16, 128], BF16, kind="Internal").ap() for i in range(NL)]

    with ExitStack() as es:
        E = es.enter_context
        P = Prog(nc, es)
        pe, act, dve, pool, sp = P.pe, P.act, P.dve, P.pool, P.sp

        def sb(name, shape, dt=F32):
            return E(nc.sbuf_tensor(name, list(shape), dt))

        NWS = 5
        Wr = [sb("Wr%d" % i, [128, 8, 128], BF16) for i in range(NWS)]; bWr = [Buf("Wr%d" % i) for i in range(NWS)]
        Wbig = sb("Wbig", [128, 4, 8, 128], BF16); bWbig = Buf("Wbig")
        WO = sb("WO", [128, 16384], BF16); bWO = Buf("WO")
        WOf = WO[:].bitcast(F32); WOi = WO[:].bitcast(I32)
        WO4 = WO[:].rearrange("p (o k c) -> p o k c", o=8, k=16)
        bWdL = [Buf("wbf_dram%d" % i) for i in range(NL)]; bWOdL = [Buf("wobf_dram%d" % i) for i in range(NL)]; wctr = [0, 0, 0]
        CW = 640
        cst = [WOf[:, 4608 + i * 640:4608 + (i + 1) * 640] for i in range(2)]; bCst = [Buf("cst0"), Buf("cst1")]
        csb = [WO[:, 12288 + i * 640:12288 + (i + 1) * 640] for i in range(2)]; bCsb = [Buf("csb0"), Buf("csb1")]
        WG = sb("WG", [128, 4, 512], BF16); bWG = Buf("WG")
        WA = sb("WA", [16, 256], BF16); bWA = Buf("WA")
        BA = sb("BA", [1, 256], BF16); bBA = Buf("BA")
        ones_bf = sb("ones_bf", [128, 128], BF16); bOnes = Buf("ones")
        onesA = sb("onesA", [128, 128], BF16); onesB = sb("onesB", [128, 128], BF16)
        ones_row = sb("ones_row", [1, 128], BF16)
        xr = sb("xr", [128, 8, 128]); bXr = Buf("xr")
        flgs = sb("flgs", [128, 2]); bFlg = Buf("flg")
        bSnd = [Buf("snd%d" % i) for i in range(3)]; bRcv = [Buf("rcv%d" % i) for i in range(3)]
        one11 = sb("one11", [1, 1], F32)
        epsc = sb("epsc", [128, 1]); onec = sb("onec", [128, 1])
        bDram = [Buf("dram%d" % i) for i in range(NB)]
        triu = sb("triu", [128, 128]); trils = sb("trils", [128, 128]); bTri = Buf("tri")
        masku4 = sb("masku4", [128, 4, 128]); bMU = Buf("mu")
        swam = sb("swam", [128, 8, 4, 128], BF16); bSM = Buf("sm")
        iota = sb("iota", [128, 128]); onesz = sb("onesz", [128, 128]); bIo = Buf("iota")
        cTs = sb("cTs", [128, 8]); bcT = Buf("cT")
        modrow = sb("modrow", [1, 256]); bModrow = Buf("modrow")
        bmrow = sb("bmrow", [1, 256]); bBmrow = Buf("bmrow")
        wmst = sb("wmst", [128, 8, 256]); bWmst = Buf("wmst")
        modc = sb("modc", [128, 24]); bModc = Buf("modc")
        gp = sb("gp", [128, 8]); gpo = sb("gpo", [128, 8]); bGp = Buf("gp")
        gs = sb("gs", [128, 8]); gg = sb("gg", [128, 8]); bGs = Buf("gs")
        snk = sb("snk", [128, 8]); esnk = sb("esnk", [128, 8]); bSnk = Buf("snk")
        gglas = sb("gglas", [128, 4]); s5ds = sb("s5ds", [128, 4]); bglus = sb("bglus", [128, 4]); bSm = Buf("small")
        colp = sb("colp", [128, 3, 16]); bColp = Buf("colp")
        colw = sb("colw", [128, 12, 16]); bColw = Buf("colw")
        magc = sb("magc", [128, 16]); thc = sb("thc", [128, 16]); bMag = Buf("mag")
        big = [WOf[:, 2304 + i * 256:2304 + (i + 1) * 256] for i in range(6)]; bBig = [Buf("big%d" % i) for i in range(6)]
        bigi = WOi[:, 5888:6144]; bBigi = Buf("bigi")
        rep = WOf[:, 3840:4608].rearrange("p (a b) -> p a b", a=3); bRep = Buf("rep")
        wrep = [WOf[:, i * 256:(i + 1) * 256] for i in range(9)]; bWrep = Buf("wrep")
        BbT = sb("BbT", [128, 2, 16, 128], BF16); bBbT = Buf("BbT")
        CmT = sb("CmT", [128, 2, 16, 128], BF16); bCmT = Buf("CmT"); bCmT1 = Buf("CmT1")
        crT = sb("crT", [128, 16, 128]); srT = sb("srT", [128, 16, 128]); rzT = sb("rzT", [128, 16, 128]); bTab = Buf("tab")
        carry = sb("carry", [128, 2, 16]); bCarry = Buf("carry")
        ctmp = sb("ctmp", [128, 2, 16]); bCtmp = Buf("ctmp")
        xb = [sb("xb%d" % i, [128, 8, 128]) for i in range(2)]; bXb = [Buf("xb0"), Buf("xb1")]
        sq = sb("sq", [128, 8, 128], BF16); bSq = Buf("sq")
        rstd = sb("rstd", [128, 128]); bRstd = Buf("rstd")
        xn = sb("xn", [128, 8, 128]); bXn = Buf("xn")
        hT = sb("hT", [128, 8, 128], BF16); bHT = Buf("hT")
        mixT = sb("mixT", [128, 16, 128], BF16); bMix = [Buf("mix%d" % i) for i in range(16)]
        ob0 = sb("ob0", [128, 8, 128]); ob = [ob0, ob0]; bOb0 = Buf("ob0"); bOb = [bOb0, bOb0]
        qT = sb("qT", [128, 16, 128], BF16); bQT = Buf("qT")
        kT = [sb("kT%d" % i, [128, 2, 128], BF16) for i in range(2)]; bKT = [Buf("kT0"), Buf("kT1")]
        Vp = [sb("Vp%d" % i, [128, 4, 128], BF16) for i in range(2)]; bVp = [Buf("Vp0"), Buf("Vp1")]
        pexp = sb("pexp", [128, 4, 128]); bPexp = Buf("pexp")
        pT = sb("pT", [128, 4, 128], BF16); bPT = Buf("pT")
        rden = sb("rden", [128, 128]); bRden = Buf("rden")
        onrm = sb("onrm", [128, 128]); bOnrm = Buf("onrm")
        sg = sb("sg", [128, 8, 128]); bSg = Buf("sg")
        lrT = sb("lrT", [16, 128], BF16); bLrT = Buf("lrT")
        lap = sb("lap", [128, 256]); bLap = Buf("lap")
        EqT = sb("EqT", [128, 2, 128]); EkT = sb("EkT", [128, 2, 128]); Er = sb("Er", [128, 256]); bEx = Buf("Ex")
        qtl = sb("qtl", [128, 4, 128], BF16); ktl = sb("ktl", [128, 2, 128], BF16); bQK = Buf("qk")
        kpr = sb("kpr", [128, 256], BF16); vtok = sb("vtok", [128, 512], BF16); bKV = Buf("kv")
        attn = sb("attn", [128, 4, 128], BF16); bAttn = Buf("attn")
        Sst = sb("Sst", [128, 2, 128]); Sbf = sb("Sbf", [128, 2, 128], BF16); bS = Buf("S"); bSb = Buf("Sb")
        gsq = sb("gsq", [128, 4, 128], BF16); bGsq = Buf("gsq")
        grs = sb("grs", [128, 4, 128]); bGrs = Buf("grs")
        gon = sb("gon", [128, 4, 128]); bGon = Buf("gon")
        gsg = sb("gsg", [128, 4, 128]); bGsg = Buf("gsg")
        uTb = sb("uTb", [128, 4, 128], BF16); uTf = sb("uTf", [128, 4, 128]); bU = Buf("u")
        wk = [sb("wk%d" % i, [128, 4, 128]) for i in range(4)]; bWk = [Buf("wk%d" % i) for i in range(4)]
        hb = sb("hb", [128, 4, 4, 128], BF16); bHb = Buf("hb")
        z = sb("z", [128, 4, 128]); z2 = sb("z2", [128, 4, 128]); bZ = Buf("z"); bZ2 = Buf("z2")
        ybf = sb("ybf", [128, 4, 128], BF16); yf = sb("yf", [128, 4, 128]); bY = Buf("y")
        csg = sb("csg", [128, 4, 128]); bCsg = Buf("csg")
        gsig = sb("gsig", [128, 4, 128]); bGsig = Buf("gsig")
        ps = E(nc.psum_tensor("ps", [128, 8, 512], F32))
        bP = [Buf("psum%d" % i) for i in range(8)]

        def mm(out, lhsT, rhs, start, stop, reads, writes, inc=False):
            P.op(pe, lambda e: e.matmul(out, lhsT=lhsT, rhs=rhs, start=start, stop=stop), reads, writes, inc=inc)

        def A(out, in_, func, reads, writes, bias=None, scale=None):
            kw = {}
            if bias is not None:
                kw["bias"] = bias
            if scale is not None:
                kw["scale"] = scale
            P.op(act, lambda e: e.activation(out=out, in_=in_, func=func, **kw), reads, writes)

        def TT(eng, out, in0, in1, op, reads, writes):
            P.op(eng, lambda e: e.tensor_tensor(out=out, in0=in0, in1=in1, op=op), reads, writes)

        def TS(eng, out, in0, s1, s2, op0, op1, reads, writes):
            if op1 is None:
                P.op(eng, lambda e: e.tensor_scalar(out=out, in0=in0, scalar1=s1, scalar2=None, op0=op0), reads, writes)
            else:
                P.op(eng, lambda e: e.tensor_scalar(out=out, in0=in0, scalar1=s1, scalar2=s2, op0=op0, op1=op1), reads, writes)

        def STT(out, in0, scalar, in1, op0, op1, reads, writes):
            P.op(dve, lambda e: e.scalar_tensor_tensor(out=out, in0=in0, scalar=scalar, in1=in1, op0=op0, op1=op1), reads, writes)

        def CP(eng, out, in_, reads, writes):
            if eng is act:
                P.op(eng, lambda e: e.activation(out=out, in_=in_, func=AF.Copy), reads, writes)
            else:
                P.op(eng, lambda e: e.tensor_copy(out=out, in_=in_), reads, writes)

        def MS(eng, ap, val, writes):
            P.op(eng, lambda e: e.memset(ap, val), (), writes)

        def f2(ap):
            return ap.rearrange("p a b -> p (a b)")

        def conv_jobs(l):
            jobs = []
            for r in range(8):
                for p_ in range(8):
                    jobs.append((win[l][r * 128:(r + 1) * 128, p_ * CW:(p_ + 1) * CW],
                                 wbfL[l][p_ * 5:(p_ + 1) * 5, :, r, :].rearrange("t p c -> p t c"), CW, bWdL[l], 5))
                jobs.append((win[l][r * 128:(r + 1) * 128, 5120:5136], wlrL[l][:, r, :], 16, bWdL[l], 0))
            for r in range(16):
                for p_ in range(2):
                    jobs.append((wout[l][r * 128:(r + 1) * 128, p_ * 512:(p_ + 1) * 512],
                                 wobfL[l][p_ * 4:(p_ + 1) * 4, :, r, :].rearrange("o p c -> p o c"), 512, bWOdL[l], 4))
            return jobs

        def conv_in(job):
            i = wctr[2] % 2; wctr[2] += 1
            src, dst, ncol, dbuf, nt = job
            P.dma(sp, cst[i][:, 0:ncol], src, (), [bCst[i]], bCst[i])
            return i

        def conv_out(job, i, eng, q=None):
            q = q or sp
            src, dst, ncol, dbuf, nt = job
            CP(eng, csb[i][:, 0:ncol], cst[i][:, 0:ncol], [bCst[i]], [bCsb[i]])
            srcv = csb[i][:, 0:ncol] if nt == 0 else csb[i][:, 0:ncol].rearrange("p (t c) -> p t c", t=nt)
            P.dma(q, dst, srcv, [bCsb[i]], [dbuf], dbuf)

        def load_cast(dst_ap, src_ap, ncol, eng, dbuf):
            i = wctr[2] % 2; wctr[2] += 1
            P.dma(sp, cst[i][:, 0:ncol], src_ap, (), [bCst[i]], bCst[i])
            CP(eng, dst_ap, cst[i][:, 0:ncol], [bCst[i]], [dbuf])

        P.dma(sp, triu[:], c_triu, (), [bTri], bTri)
        P.dma(sp, trils[:], c_trils, (), [bTri], bTri)
        P.dma(sp, f2(masku4[:]), c_masku4, (), [bMU], bMU)
        for q_ in range(8):
            load_cast(swam[:].rearrange("p a b c -> p (a b c)")[:, q_ * 512:(q_ + 1) * 512], c_swam[:, q_ * 512:(q_ + 1) * 512], 512, (dve, act)[q_ % 2], bSM)
        P.dma(sp, iota[:], c_iota, (), [bIo], bIo)
        P.dma(sp, onesz[:], c_onesz, (), [bIo], bIo)
        P.dma(sp, cTs[:], cT, (), [bcT], bcT)
        P.dma(sp, flgs[:], flg, (), [bFlg], bFlg)
        MS(dve, ones_bf[:], 1.0, [bOnes])
        MS(dve, onesA[:], 0.0, [bOnes]); MS(dve, onesB[:], 0.0, [bOnes])
        MS(dve, onesA[:, 0:64], 1.0, [bOnes]); MS(dve, onesB[:, 64:128], 1.0, [bOnes])
        MS(dve, ones_row[:], 1.0, [bOnes]); MS(dve, one11[:], 1.0, [bOnes])
        MS(dve, epsc[:], EPS, [bOnes]); MS(dve, onec[:], 1.0, [bOnes])
        A(cTs[:], cTs[:], AF.Silu, [bcT], [bcT])

        def range_sin(eng_list, out, ang, tmp, tmpi, n, rb, wb_):
            TS(dve, tmp, ang, 1.0 / TWO_PI, None, ALU.mult, None, rb, wb_)
            CP(dve, tmpi, tmp, wb_, wb_)
            CP(dve, tmp, tmpi, wb_, wb_)
            STT(tmp, tmp, -TWO_PI, ang, ALU.mult, ALU.add, rb + wb_, wb_)
            TS(dve, out, tmp, math.pi, -TWO_PI, ALU.is_gt, ALU.mult, wb_, wb_)
            TT(dve, tmp, tmp, out, ALU.add, wb_, wb_)
            TS(dve, out, tmp, -math.pi, TWO_PI, ALU.is_lt, ALU.mult, wb_, wb_)
            TT(dve, tmp, tmp, out, ALU.add, wb_, wb_)
            A(out, tmp, AF.Sin, wb_, wb_)

        try:
          if STOP == 1:
            raise _Stop()
          for l in range(NL):
            src_x = xT if l == 0 else x1T
            dst_x = outT if l == NL - 1 else x1T
            wbf, wlr, wobf, bWd, bWOd = wbfL[l], wlrL[l], wobfL[l], bWdL[l], bWOdL[l]
            P.dma(sp, gp[:], gpre[l], (), [bGp], bGp)
            P.dma(sp, gpo[:], gpost[l], (), [bGp], bGp)
            P.dma(sp, snk[:], sinks[l], (), [bSnk], bSnk)
            P.dma(sp, gglas[:], ggla[l], (), [bSm], bSm)
            P.dma(sp, s5ds[:], s5d[l], (), [bSm], bSm)
            P.dma(sp, bglus[:], bglu[l], (), [bSm], bSm)
            P.dma(sp, colp[:], s5col[l], (), [bColp], bColp)
            if l == 0:
                jobs0 = conv_jobs(0)
                engs = (pool, pool, pool)
                pend = [conv_in(jobs0[0])]
                for ji in range(len(jobs0)):
                    if ji + 1 < len(jobs0):
                        pend.append(conv_in(jobs0[ji + 1]))
                    conv_out(jobs0[ji], pend[ji], engs[ji % 3], q=pool)
            bgjobs = conv_jobs(l + 1) if l + 1 < NL else []
            bgpend = []

            def bg_service(flush=False):
                while True:
                    for jb, slot in bgpend:
                        conv_out(jb, slot, pool, q=pool)
                    del bgpend[:]
                    for _ in range(2):
                        if bgjobs:
                            jb = bgjobs.pop(0)
                            bgpend.append((jb, conv_in(jb)))
                    if not flush or not bgpend:
                        break
            for k_ in range(4):
                load_cast(WG[:, k_, :], wglu[l][k_ * 128:(k_ + 1) * 128, :], 512, pool, bWG)
            P.dma(pool, WA[:], walpha[l], (), [bWA], bWA)
            P.dma(pool, BA[:], balpha[l], (), [bBA], bBA)
            def abar_calc(are, aim, ldt, w, n, rb, wbuf, tmpi):
                dt_, th, mag, sn, cs, t0, t1, fr, fi = w[:9]
                A(dt_, ldt, AF.Exp, rb, wbuf)
                TT(dve, th, aim, dt_, ALU.mult, rb + wbuf, wbuf)
                TT(dve, t0, are, dt_, ALU.mult, rb + wbuf, wbuf)
                A(mag, t0, AF.Exp, wbuf, wbuf)
                range_sin(None, sn, th, t0, tmpi, n, wbuf, wbuf)
                TS(dve, t1, th, math.pi / 2, None, ALU.add, None, wbuf, wbuf)
                range_sin(None, cs, t1, t0, tmpi, n, wbuf, wbuf)
                TT(dve, cs, cs, mag, ALU.mult, wbuf, wbuf)
                TT(dve, sn, sn, mag, ALU.mult, wbuf, wbuf)
                TT(dve, t0, are, are, ALU.mult, rb + wbuf, wbuf)
                TT(dve, t1, aim, aim, ALU.mult, rb + wbuf, wbuf)
                TT(dve, t0, t0, t1, ALU.add, wbuf, wbuf)
                P.op(dve, lambda e: e.reciprocal(out=t0, in_=t0), wbuf, wbuf)
                TS(dve, t1, cs, -1.0, None, ALU.add, None, wbuf, wbuf)
                TT(dve, fr, t1, are, ALU.mult, rb + wbuf, wbuf)
                TT(dve, dt_, sn, aim, ALU.mult, rb + wbuf, wbuf)
                TT(dve, fr, fr, dt_, ALU.add, wbuf, wbuf)
                TT(dve, fr, fr, t0, ALU.mult, wbuf, wbuf)
                TT(dve, fi, sn, are, ALU.mult, rb + wbuf, wbuf)
                TT(dve, dt_, t1, aim, ALU.mult, rb + wbuf, wbuf)
                TT(dve, fi, fi, dt_, ALU.subtract, wbuf, wbuf)
                TT(dve, fi, fi, t0, ALU.mult, wbuf, wbuf)
                return cs, sn, fr, fi, mag, th

            cw = [colw[:, i, :] for i in range(9)]
            _, _, _, _, magv, thv = abar_calc(colp[:, 0, :], colp[:, 1, :], colp[:, 2, :], cw, 16, [bColp], [bColw], bigi[:, 0:16])
            CP(dve, magc[:], magv, [bColw], [bMag]); CP(dve, thc[:], thv, [bColw], [bMag])
            for q_ in range(4):
                load_cast(CmT[:, 0].rearrange("p a b -> p (a b)")[:, q_ * 512:(q_ + 1) * 512], cpad[l, 0][:, q_ * 512:(q_ + 1) * 512], 512, pool, bCmT)
            for sl in range(8):
                cs_ = slice(sl * 256, (sl + 1) * 256)
                for jj in range(2):
                    j = sl * 2 + jj
                    TS(dve, big[0][:, jj * 128:(jj + 1) * 128], iota[:], thc[:, j:j + 1], None, ALU.mult, None, [bIo, bMag], [bBig[0]])
                    TS(dve, rzT[:, j, :], onesz[:], magc[:, j:j + 1], None, ALU.mult, None, [bIo, bMag], [bTab])
                range_sin(None, f2(srT[:])[:, cs_], big[0][:], big[1][:], bigi[:], 256, [bBig[0]], [bBig[1], bBigi, bTab])
                TS(dve, big[2][:], big[0][:], math.pi / 2, None, ALU.add, None, [bBig[0]], [bBig[2]])
                range_sin(None, f2(crT[:])[:, cs_], big[2][:], big[1][:], bigi[:], 256, [bBig[2]], [bBig[1], bBigi, bTab])
                P.dma(act, rep[:], s5rep[l][:, :, cs_], (), [bRep], bRep)
                _, _, frv, fiv, _, _ = abar_calc(rep[:, 0, :], rep[:, 1, :], rep[:, 2, :], [w_[:] for w_ in wrep], 256, [bRep], [bWrep], bigi[:])
                P.dma(act, big[3][:], btpad[l, 0][:, cs_], (), [bBig[3]], bBig[3])
                P.dma(act, big[4][:], btpad[l, 1][:, cs_], (), [bBig[4]], bBig[4])
                TT(dve, big[0][:], frv, big[3][:], ALU.mult, [bWrep, bBig[3]], [bBig[0]])
                TT(dve, big[1][:], fiv, big[4][:], ALU.mult, [bWrep, bBig[4]], [bBig[1]])
                TT(dve, BbT[:, 0].rearrange("p a b -> p (a b)")[:, cs_], big[0][:], big[1][:], ALU.subtract, [bBig[0], bBig[1]], [bBbT])
                TT(dve, big[0][:], frv, big[4][:], ALU.mult, [bWrep, bBig[4]], [bBig[0]])
                TT(dve, big[1][:], fiv, big[3][:], ALU.mult, [bWrep, bBig[3]], [bBig[1]])
                TT(dve, BbT[:, 1].rearrange("p a b -> p (a b)")[:, cs_], big[0][:], big[1][:], ALU.add, [bBig[0], bBig[1]], [bBbT])
                P.dma(act, big[5][:], cpad[l, 1][:, cs_], (), [bBig[5]], bBig[5])
                TS(dve, CmT[:, 1].rearrange("p a b -> p (a b)")[:, cs_], big[5][:], -1.0, None, ALU.mult, None, [bBig[5]], [bCmT1])
            if STOP == 3:
                raise _Stop()
            for ch in range(12):
                P.dma(act, wmst[:], wmod[l][:, ch * 256:(ch + 1) * 256].rearrange("(k p) c -> p k c", p=128), (), [bWmst], bWmst)
                P.dma(act, bmrow[:], bmod[l][:, ch * 256:(ch + 1) * 256], (), [bBmrow], bBmrow)
                for k in range(8):
                    mm(ps[0:1, 0, 0:256], cTs[:, k:k + 1], wmst[:, k, :], k == 0, False, [bcT, bWmst], [bP[0]])
                mm(ps[0:1, 0, 0:256], one11[:], bmrow[:], False, True, [bOnes, bBmrow], [bP[0]], inc=True)
                CP(act, modrow[:], ps[0:1, 0, 0:256], [bP[0]], [bModrow])
                for tt in range(2):
                    t = ch * 2 + tt
                    mm(ps[:, 1, t:t + 1], modrow[0:1, tt * 128:(tt + 1) * 128], one11[:], True, True, [bModrow, bOnes], [bP[1]], inc=True)
            CP(dve, modc[:], ps[:, 1, 0:24], [bP[1]], [bModc])
            if STOP == 2:
                raise _Stop()
            STT(gs[:], modc[:, 8:16], 1.0, gp[:], ALU.add, ALU.mult, [bModc, bGp], [bGs])
            TT(dve, gg[:], modc[:, 16:24], gpo[:], ALU.mult, [bModc, bGp], [bGs])
            A(esnk[:], snk[:], AF.Exp, [bSnk], [bSnk])
            MS(dve, carry[:], 0.0, [bCarry])
            MS(dve, qT[:], 0.0, [bQT])
            MS(dve, qtl[:], 0.0, [bQK])
            MS(dve, Sst[:], 0.0, [bS]); MS(dve, Sbf[:], 0.0, [bSb])
            for i in range(2):
                MS(dve, kT[i][:], 0.0, [bKT[i]]); MS(dve, Vp[i][:], 0.0, [bVp[i]])

            def prenorm(n):
                cur = n % 2
                tsl = slice(n * 128, (n + 1) * 128)
                xt = xb[cur]
                P.dma(sp, xt[:], src_x[:, tsl].rearrange("(k p) t -> p k t", p=128), [], [bXb[cur]], bXb[cur])
                if n >= LAG:
                    sl = (n - LAG) % 3
                    P.dma(sp, xr[:], rcvL[sl][0:D, :].rearrange("(k p) t -> p k t", p=128), [bRcv[sl]], [bXr], bXr)
                    STT(xt[:].rearrange("p a b -> p (a b)"), xr[:].rearrange("p a b -> p (a b)"), flgs[:, 1:2], xt[:].rearrange("p a b -> p (a b)"),
                        ALU.mult, ALU.add, [bXr, bFlg, bXb[cur]], [bXb[cur]])
                A(sq[:], xt[:], AF.Square, [bXb[cur]], [bSq])
                for k in range(8):
                    mm(ps[:, 2, 256:384], ones_bf[:], sq[:, k, :], k == 0, k == 7, [bOnes, bSq], [bP[2]], inc=(k == 7))
                A(rstd[:], ps[:, 2, 256:384], AF.Ln, [bP[2], bOnes], [bRstd], bias=epsc[:], scale=1.0 / D)
                A(rstd[:], rstd[:], AF.Exp, [bRstd], [bRstd], scale=-0.5)
                TT(dve, xn[:], xt[:], rstd[:].unsqueeze(1).broadcast_to([128, 8, 128]), ALU.mult, [bXb[cur], bRstd], [bXn])
                for k in range(8):
                    TS(dve, hT[:, k, :], xn[:, k, :], gs[:, k:k + 1], modc[:, k:k + 1], ALU.mult, ALU.add, [bXn, bGs, bModc], [bHT])

            scratchB = [bWrep, bRep, bBigi] + bBig + bCst + bCsb
            for o in range(8):
                P.dma(sp, WO4[:, o], wobf[o], [bWOd], [bWO] + scratchB, bWO)
            prenorm(0)
            for n in range(NB):
                cur = n % 2
                prv = 1 - cur
                tsl = slice(n * 128, (n + 1) * 128)
                xt = xb[cur]
                if n == LAG:
                    fa = flgs[:, 0:1]
                    TS(dve, carry[:].rearrange("p a b -> p (a b)"), carry[:].rearrange("p a b -> p (a b)"), fa, None, ALU.mult, None, [bCarry, bFlg], [bCarry])
                    TS(dve, f2(Sst[:]), f2(Sst[:]), fa, None, ALU.mult, None, [bS, bFlg], [bS])
                    CP(act, Sbf[:], Sst[:], [bS], [bSb])
                    TS(dve, f2(kT[prv][:]), f2(kT[prv][:]), fa, None, ALU.mult, None, [bKT[prv], bFlg], [bKT[prv]])
                    TS(dve, f2(Vp[prv][:]), f2(Vp[prv][:]), fa, None, ALU.mult, None, [bVp[prv], bFlg], [bVp[prv]])

                def wslot(c0, ncols):
                    if ncols <= 128:
                        i = wctr[0] % NWS; wctr[0] += 1
                        tile_, tb = Wr[i], bWr[i]
                        if ncols == 128:
                            P.dma(sp, tile_[:], wbf[c0 // 128], [bWd], [tb], tb)
                        else:
                            P.dma(sp, tile_[:, :, 0:ncols], wlr, [bWd], [tb], tb)
                        return (lambda k: tile_[:, k, 0:ncols]), tb
                    nt = ncols // 128
                    t0 = c0 // 128
                    P.dma(sp, Wbig[:, 0:nt].rearrange("p t k c -> p t (k c)"),
                          wbf[t0:t0 + nt].rearrange("t p k c -> p t (k c)"), [bWd], [bWbig], bWbig)
                    return (lambda k: Wbig[:, 0:nt, k, :]), bWbig

                def proj(pout, c0, ncols, pbuf, inc_last=True):
                    wt, tb = wslot(c0, ncols)
                    for k in range(8):
                        mm(pout, wt(k), hT[:, k, :], k == 0, k == 7, [tb, bHT], [pbuf], inc=(k == 7 and inc_last))

                def proj_tok(pout, c0, ncols, pbuf):
                    wt, tb = wslot(c0, ncols)
                    for k in range(8):
                        mm(pout, hT[:, k, :], wt(k), k == 0, k == 7, [tb, bHT], [pbuf], inc=(k == 7))

                def gen_swa():
                    for half in range(2):
                        for j in range(4):
                            proj(ps[:, half, j * 128:(j + 1) * 128], O_AQ + (half * 4 + j) * 128, 128, bP[half])
                        for jq in range(4):
                            jt = half * 4 + jq
                            CP(act, qT[0:64, 2 * jt, :], ps[0:64, half, jq * 128:(jq + 1) * 128], [bP[half]], [bQT])
                            CP(act, qT[64:128, 2 * jt + 1, :], ps[64:128, half, jq * 128:(jq + 1) * 128], [bP[half]], [bQT])
                        yield
                    for j in range(2):
                        proj(ps[:, 2, j * 128:(j + 1) * 128], O_AK + j * 128, 128, bP[2])
                    CP(act, kT[cur][:], ps[:, 2, 0:256].rearrange("p (a b) -> p a b", a=2), [bP[2]], [bKT[cur]])
                    yield
                    proj_tok(ps[:, 2, 256:512], O_AV, 256, bP[2])
                    for g in range(4):
                        c0 = 0 if g % 2 == 0 else 64
                        CP(dve, Vp[cur][:, g, c0:c0 + 64], ps[:, 2, 256 + g * 64:256 + (g + 1) * 64], [bP[2]], [bVp[cur]])
                    yield
                    for half in range(2):
                        for j in range(4):
                            proj(ps[:, half, j * 128:(j + 1) * 128], O_AG + (half * 4 + j) * 128, 128, bP[half])
                        A(sg[:, half * 4:(half + 1) * 4, :], ps[:, half, :].rearrange("p (a b) -> p a b", a=4), AF.Silu, [bP[half]], [bSg])
                        yield
                    nq = 4 if n > 0 else 2

                    def scores(j):
                        kt = 0 if j < 4 else 1
                        pb = j % 2
                        for idx in range(nq):
                            ksrc = kT[cur] if idx < 2 else kT[prv]
                            kb = bKT[cur] if idx < 2 else bKT[prv]
                            mm(ps[:, pb, idx * 128:(idx + 1) * 128], ksrc[:, kt, :], qT[:, 2 * j + (idx % 2), :], True, True,
                               [kb, bQT], [bP[pb]], inc=(idx == nq - 1))
                    scores(0)
                    for j in range(8):
                        gA, gB = (0, 1) if j < 4 else (2, 3)
                        pb = j % 2
                        A(f2(pexp[:])[:, 0:nq * 128], ps[:, pb, 0:nq * 128], AF.Exp, [bP[pb]], [bPexp], scale=0.125)
                        TT(dve, f2(pT[:])[:, 0:nq * 128], f2(pexp[:])[:, 0:nq * 128], swam[:, j].rearrange("p a b -> p (a b)")[:, 0:nq * 128],
                           ALU.mult, [bPexp, bSM], [bPT])
                        if n == LAG:
                            TS(dve, f2(pT[:])[:, 256:512], f2(pT[:])[:, 256:512], flgs[:, 0:1], None, ALU.mult, None, [bPT, bFlg], [bPT])
                        if j + 1 < 8:
                            scores(j + 1)
                        for idx in range(nq):
                            g = (gA if idx % 2 == 0 else gB)
                            vsrc = Vp[cur] if idx < 2 else Vp[prv]
                            vb = bVp[cur] if idx < 2 else bVp[prv]
                            mm(ps[:, 2, 0:128], vsrc[:, g, :], pT[:, idx, :], idx == 0, idx == nq - 1, [vb, bPT], [bP[2]])
                        for idx in range(nq):
                            mm(ps[:, 2, 128:256], (onesA if idx % 2 == 0 else onesB)[:], pT[:, idx, :], idx == 0, idx == nq - 1,
                               [bOnes, bPT], [bP[2]], inc=(idx == nq - 1))
                        A(rden[:], ps[:, 2, 128:256], AF.Ln, [bP[2], bSnk], [bRden], bias=esnk[:, j:j + 1], scale=1.0)
                        A(rden[:], rden[:], AF.Exp, [bRden], [bRden], scale=-1.0)
                        TT(dve, onrm[:], ps[:, 2, 0:128], rden[:], ALU.mult, [bP[2], bRden], [bOnrm])
                        TT(dve, mixT[:, j, :], onrm[:], sg[:, j, :], ALU.mult, [bOnrm, bSg], [bMix[j]])
                        yield

                def gen_gla():
                    proj(ps[0:16, 3, 0:128], O_LR, 16, bP[3])
                    CP(act, lrT[:], ps[0:16, 3, 0:128], [bP[3]], [bLrT])
                    mm(ps[:, 4, 0:256], lrT[:], WA[:], True, False, [bLrT, bWA], [bP[4]])
                    mm(ps[:, 4, 0:256], ones_row[:], BA[:], False, True, [bOnes, bBA], [bP[4]], inc=True)
                    A(lap[:], ps[:, 4, 0:256], AF.Exp, [bP[4]], [bLap], scale=-1.0)
                    A(lap[:], lap[:], AF.Ln, [bLap, bOnes], [bLap], bias=onec[:], scale=1.0)
                    yield
                    for t in range(2):
                        mm(ps[:, 3, t * 128:(t + 1) * 128], lap[:, t * 128:(t + 1) * 128], triu[:], True, True, [bLap, bTri], [bP[3]], inc=(t == 1))
                    mm(ps[:, 4, 256:512], trils[:], lap[:], True, True, [bLap, bTri], [bP[4]], inc=True)
                    A(f2(EqT[:]), ps[:, 3, 0:256], AF.Exp, [bP[3]], [bEx], scale=-1.0 / 16)
                    A(f2(EkT[:]), ps[:, 3, 0:256], AF.Exp, [bP[3]], [bEx], scale=1.0 / 16)
                    A(Er[:], ps[:, 4, 256:512], AF.Exp, [bP[4]], [bEx], scale=-1.0 / 16)
                    yield
                    for t in range(2):
                        proj(ps[:, 3, t * 128:(t + 1) * 128], O_BQ + t * 128, 128, bP[3])
                    for t in range(2):
                        proj(ps[:, 3, 256 + t * 128:256 + (t + 1) * 128], O_BK + t * 128, 128, bP[3])
                    for h in range(4):
                        t, r0 = h // 2, (h % 2) * 64
                        STT(qtl[r0:r0 + 64, h, :], ps[r0:r0 + 64, 3, t * 128:(t + 1) * 128], 0.125, EqT[r0:r0 + 64, t, :], ALU.mult, ALU.mult, [bP[3], bEx], [bQK])
                    TT(dve, f2(ktl[:]), ps[:, 3, 256:512], f2(EkT[:]), ALU.mult, [bP[3], bEx], [bQK])
                    yield
                    proj_tok(ps[:, 4, 0:256], O_BK, 256, bP[4])
                    TT(dve, kpr[:], ps[:, 4, 0:256], Er[:], ALU.mult, [bP[4], bEx], [bKV])
                    yield
                    proj_tok(ps[:, 3, :], O_BV, 512, bP[3])
                    CP(act, vtok[:], ps[:, 3, :], [bP[3]], [bKV])
                    yield
                    for h in range(4):
                        t = h // 2
                        mm(ps[:, 4, h * 128:(h + 1) * 128], ktl[:, t, :], qtl[:, h, :], True, True, [bQK], [bP[4]], inc=(h == 3))
                    TT(dve, f2(attn[:]), ps[:, 4, :], f2(masku4[:]), ALU.mult, [bP[4], bMU], [bAttn])
                    yield
                    for h in range(4):
                        t = h // 2
                        mm(ps[:, 3, h * 128:(h + 1) * 128], vtok[:, h * 128:(h + 1) * 128], attn[:, h, :], True, False, [bKV, bAttn], [bP[3]])
                        mm(ps[:, 3, h * 128:(h + 1) * 128], Sbf[:, t, :], qtl[:, h, :], False, True, [bSb, bQK], [bP[3]], inc=(h == 3))
                    for h in range(4):
                        t = h // 2
                        mm(ps[:, 4, h * 128:(h + 1) * 128], kpr[:, t * 128:(t + 1) * 128], vtok[:, h * 128:(h + 1) * 128], True, True, [bKV], [bP[4]], inc=(h == 3))
                    for h in range(4):
                        t, r0 = h // 2, (h % 2) * 64
                        STT(Sst[r0:r0 + 64, t, :], Sst[r0:r0 + 64, t, :], EqT[r0:r0 + 64, t, 127:128], ps[r0:r0 + 64, 4, h * 128:(h + 1) * 128],
                            ALU.mult, ALU.add, [bS, bEx, bP[4]], [bS])
                    CP(act, Sbf[:], Sst[:], [bS], [bSb])
                    yield
                    A(f2(gsq[:]), ps[:, 3, :], AF.Square, [bP[3]], [bGsq])
                    for h in range(4):
                        mm(ps[:, 4, h * 128:(h + 1) * 128], ones_bf[:], gsq[:, h, :], True, True, [bOnes, bGsq], [bP[4]], inc=(h == 3))
                    A(f2(grs[:]), ps[:, 4, :], AF.Ln, [bP[4], bOnes], [bGrs], bias=epsc[:], scale=1.0 / 128)
                    A(f2(grs[:]), f2(grs[:]), AF.Exp, [bGrs], [bGrs], scale=-0.5)
                    TT(dve, f2(gon[:]), ps[:, 3, :], f2(grs[:]), ALU.mult, [bP[3], bGrs], [bGon])
                    yield
                    for h in range(4):
                        proj(ps[:, 4, h * 128:(h + 1) * 128], O_BG + h * 128, 128, bP[4])
                    A(f2(gsg[:]), ps[:, 4, :], AF.Silu, [bP[4]], [bGsg])
                    for h in range(4):
                        STT(mixT[:, 8 + h, :], gon[:, h, :], gglas[:, h:h + 1], gsg[:, h, :], ALU.mult, ALU.mult, [bGon, bSm, bGsg], [bMix[8 + h]])
                    yield

                def gen_s5():
                    for c in range(4):
                        proj(ps[:, 5, c * 128:(c + 1) * 128], O_CU + c * 128, 128, bP[5])
                    CP(act, f2(uTb[:]), ps[:, 5, :], [bP[5]], [bU])
                    CP(act, f2(uTf[:]), ps[:, 5, :], [bP[5]], [bU])
                    yield
                    for c in range(4):
                        for jj in range(4):
                            j = c * 4 + jj
                            for pl in range(2):
                                mm(ps[:, 6 + pl, jj * 128:(jj + 1) * 128], BbT[:, pl, j, :], uTb[:, c, :], True, True,
                                   [bBbT, bU], [bP[6 + pl]], inc=(jj == 3))
                        xr = ps[:, 6, :]
                        xi = ps[:, 7, :]
                        cr = crT[:, c * 4:(c + 1) * 4, :].rearrange("p a b -> p (a b)")
                        sr = srT[:, c * 4:(c + 1) * 4, :].rearrange("p a b -> p (a b)")
                        rz = rzT[:, c * 4:(c + 1) * 4, :].rearrange("p a b -> p (a b)")
                        a_, b_, c_, d_ = [f2(w_[:]) for w_ in wk]
                        bA, bB, bC, bD = bWk
                        rX = [bP[6], bP[7]]
                        TT(dve, a_, xr, cr, ALU.mult, rX + [bTab], [bA])
                        TT(dve, b_, xi, sr, ALU.mult, rX + [bTab], [bB])
                        TT(pool, a_, a_, b_, ALU.add, [bA, bB], [bA])
                        TT(dve, b_, xi, cr, ALU.mult, rX + [bTab], [bB])
                        TT(dve, c_, xr, sr, ALU.mult, rX + [bTab], [bC])
                        TT(pool, b_, b_, c_, ALU.subtract, [bB, bC], [bB])
                        yield
                        for pl, (tile_, tb) in enumerate(((wk[0], bA), (wk[1], bB))):
                            TT(dve, ctmp[:, pl, c * 4:(c + 1) * 4], carry[:, pl, c * 4:(c + 1) * 4], magc[:, c * 4:(c + 1) * 4], ALU.mult,
                               [bCarry, bMag], [bCtmp])
                            TT(dve, tile_[:, :, 0], tile_[:, :, 0], ctmp[:, pl, c * 4:(c + 1) * 4], ALU.add, [tb, bCtmp], [tb])
                        P.op(dve, lambda e: e.tensor_tensor_scan(out=c_, data0=rz, data1=a_, initial=0.0, op0=ALU.mult, op1=ALU.add), [bTab, bA], [bC])
                        P.op(dve, lambda e: e.tensor_tensor_scan(out=d_, data0=rz, data1=b_, initial=0.0, op0=ALU.mult, op1=ALU.add), [bTab, bB], [bD])
                        yield
                        hp = [hb[:, q_].rearrange("p a b -> p (a b)") for q_ in range(4)]
                        TT(dve, hp[0], c_, cr, ALU.mult, [bC, bTab], [bHb])
                        TT(pool, hp[2], c_, sr, ALU.mult, [bC, bTab], [bHb])
                        STT(hp[1], d_, -1.0, sr, ALU.mult, ALU.mult, [bD, bTab], [bHb])
                        TT(pool, hp[3], d_, cr, ALU.mult, [bD, bTab], [bHb])
                        c4 = slice(c * 4, (c + 1) * 4)
                        grl = wk[2][:, :, 127]; gil = wk[3][:, :, 127]
                        crl = crT[:, c4, 127]; srl = srT[:, c4, 127]
                        TT(dve, ctmp[:, 0, c4], grl, crl, ALU.mult, [bC, bTab], [bCtmp])
                        TT(dve, ctmp[:, 1, c4], gil, srl, ALU.mult, [bD, bTab], [bCtmp])
                        TT(dve, carry[:, 0, c4], ctmp[:, 0, c4], ctmp[:, 1, c4], ALU.subtract, [bCtmp], [bCarry])
                        TT(dve, ctmp[:, 0, c4], grl, srl, ALU.mult, [bC, bTab], [bCtmp])
                        TT(dve, ctmp[:, 1, c4], gil, crl, ALU.mult, [bD, bTab], [bCtmp])
                        TT(dve, carry[:, 1, c4], ctmp[:, 0, c4], ctmp[:, 1, c4], ALU.add, [bCtmp], [bCarry])
                        yield
                        first = True
                        for jj in range(4):
                            j = c * 4 + jj
                            for q_ in range(4):
                                last = (jj == 3 and q_ == 3)
                                mm(ps[:, 5, c * 128:(c + 1) * 128], CmT[:, q_ // 2, j, :], hb[:, q_, jj, :], first, last, [bCmT, bCmT1, bHb], [bP[5]], inc=last)
                                first = False
                    for c in range(4):
                        STT(z[:, c, :], uTf[:, c, :], s5ds[:, c:c + 1], ps[:, 5, c * 128:(c + 1) * 128], ALU.mult, ALU.add, [bU, bSm, bP[5]], [bZ])
                    TT(dve, f2(z2[:]), f2(z[:]), f2(z[:]), ALU.mult, [bZ], [bZ2])
                    TS(dve, f2(z2[:]), f2(z2[:]), 0.044715, 1.0, ALU.mult, ALU.add, [bZ2], [bZ2])
                    TT(dve, f2(z2[:]), f2(z2[:]), f2(z[:]), ALU.mult, [bZ, bZ2], [bZ2])
                    A(f2(z2[:]), f2(z2[:]), AF.Sigmoid, [bZ2], [bZ2], scale=1.5957691216057308)
                    TT(dve, f2(yf[:]), f2(z[:]), f2(z2[:]), ALU.mult, [bZ, bZ2], [bY])
                    CP(act, f2(ybf[:]), f2(yf[:]), [bY], [bY])
                    yield
                    for co in range(4):
                        for k in range(4):
                            mm(ps[:, 6, co * 128:(co + 1) * 128], WG[:, k, co * 128:(co + 1) * 128], ybf[:, k, :], k == 0, k == 3, [bWG, bY], [bP[6]],
                               inc=(k == 3 and co == 3))
                    for co in range(4):
                        A(gsig[:, co, :], ps[:, 6, co * 128:(co + 1) * 128], AF.Sigmoid, [bP[6], bSm], [bGsig], bias=bglus[:, co:co + 1], scale=1.0)
                    yield
                    for c in range(4):
                        proj(ps[:, 7, c * 128:(c + 1) * 128], O_CG + c * 128, 128, bP[7])
                    A(f2(csg[:]), ps[:, 7, :], AF.Silu, [bP[7]], [bCsg])
                    TT(dve, f2(z[:]), f2(yf[:]), f2(gsig[:]), ALU.mult, [bY, bGsig], [bZ])
                    TT(dve, mixT[:, 12:16, :].rearrange("p a b -> p (a b)"), f2(z[:]), f2(csg[:]), ALU.mult, [bZ, bCsg], bMix[12:16])
                    yield

                gens = [gen_s5(), gen_swa(), gen_gla()] if INTERLEAVE else []
                if not INTERLEAVE:
                    for g_ in (gen_swa(), gen_gla(), gen_s5()):
                        for _ in g_:
                            pass
                while gens:
                    for g_ in list(gens):
                        try:
                            next(g_)
                        except StopIteration:
                            gens.remove(g_)

                if n + 1 < NB:
                    prenorm(n + 1)
                bg_service()
                if STOP == 7:
                    raise _Stop()
                for o in range(8):
                    bank = o // 4
                    for kt in range(16):
                        mm(ps[:, bank, (o % 4) * 128:(o % 4 + 1) * 128], WO4[:, o, kt, :], mixT[:, kt, :], kt == 0, kt == 15,
                           [bWO, bMix[kt]], [bP[bank]], inc=(kt == 15))
                yps = ps[:, 0:2, :].rearrange("p a b -> p (a b)")
                A(f2(sq[:]), yps, AF.Square, [bP[0], bP[1]], [bSq])
                for k in range(8):
                    mm(ps[:, 2, 0:128], ones_bf[:], sq[:, k, :], k == 0, k == 7, [bOnes, bSq], [bP[2]], inc=(k == 7))
                A(rstd[:], ps[:, 2, 0:128], AF.Ln, [bP[2], bOnes], [bRstd], bias=epsc[:], scale=1.0 / D)
                A(rstd[:], rstd[:], AF.Exp, [bRstd], [bRstd], scale=-0.5)
                TT(dve, xn[:], ps[:, 0:2, :].rearrange("p a (b c) -> p (a b) c", c=128),
                   rstd[:].unsqueeze(1).broadcast_to([128, 8, 128]), ALU.mult, [bP[0], bP[1], bRstd], [bXn])
                for k in range(8):
                    STT(ob[cur][:, k, :], xn[:, k, :], gg[:, k:k + 1], xt[:, k, :], ALU.mult, ALU.add, [bXn, bGs, bXb[cur]], [bOb[cur]])
                P.dma(pool, dst_x[:, tsl].rearrange("(k p) t -> p k t", p=128), ob[cur][:], [bOb[cur]], [bDram[n]], bOb[cur])
                sl = n % 3
                P.dma(pool, sndL[sl].rearrange("(k p) t -> p k t", p=128), ob[cur][:], [bOb[cur]], [bSnd[sl]], bSnd[sl])
                _coll(P, pool, groups, rcvL[sl], sndL[sl], [bSnd[sl]], [bRcv[sl]], bRcv[sl])
        except _Stop:
            pass
        if bOb0.dsem is not None:
            pool.eng.wait_ge(bOb0.dsem, bOb0.dcount)
        for b_ in bSnd + bRcv:
            if b_.dsem is not None:
                pool.eng.wait_ge(b_.dsem, b_.dcount)
    return nc


def _alibi_slopes(n):
    return np.exp2(-8.0 * np.arange(1, n + 1, dtype=np.float64) / n)


def host_prep(inputs, b, T, NL, l0=0, role=0):
    f = lambda a: np.ascontiguousarray(a, dtype=np.float32)
    inputs = dict(inputs)
    for k_ in list(inputs):
        if k_ not in ("x", "c"):
            inputs[k_] = inputs[k_][l0:l0 + NL]
    m = {}
    xt_ = np.zeros((D, T + LAG * 128), np.float32)
    if role == 0:
        xt_[:, :T] = inputs["x"][b, :T].T
    m["xT"] = xt_
    fl = np.zeros((128, 2), np.float32)
    fl[:, role] = 1.0
    m["flg"] = fl
    m["cT"] = f(inputs["c"][b].reshape(8, 128).T)
    m["wmod"] = f(inputs["w_mod"][:NL])
    m["bmod"] = f(inputs["b_mod"][:NL].reshape(NL, 1, 3 * D))
    col8 = lambda a: f(a[:NL].reshape(NL, 8, 128).transpose(0, 2, 1))
    m["gpre"] = col8(inputs["g_pre"])
    m["gpost"] = col8(inputs["g_post"])
    w_in = inputs["w_in"][:NL]
    offs = np.cumsum([0, 1024, 256, 256, 1024, 256, 256, 512, 16, 512, 512, 512])
    aq, ak, av, ag, bq, bk, bv, blr, bg, cu, cg = [w_in[:, :, offs[i]:offs[i + 1]] for i in range(11)]
    hperm = []
    for j in range(8):
        hA, hB = swa_tile_heads(j)
        hperm += list(range(hA * 64, hA * 64 + 64)) + list(range(hB * 64, hB * 64 + 64))
    hperm = np.array(hperm)
    m["win"] = f(np.concatenate([aq[:, :, hperm], ak, ag[:, :, hperm], bq, bk, bg, cu, cg, av, bv, blr], axis=2))
    wo = inputs["w_out"][:NL]
    m["wout"] = f(np.concatenate([wo[:, hperm, :], wo[:, 1024:, :]], axis=1))
    sk = inputs["attn_sinks"][:NL]
    sc = np.zeros((NL, 128, 8), np.float32)
    for j in range(8):
        hA, hB = swa_tile_heads(j)
        sc[:, 0:64, j] = sk[:, hA][:, None]
        sc[:, 64:128, j] = sk[:, hB][:, None]
    m["sinks"] = sc
    m["walpha"] = f(inputs["gla_w_alpha"][:NL])
    m["balpha"] = f(inputs["gla_b_alpha"][:NL].reshape(NL, 1, 256))
    col4 = lambda a: f(a[:NL].reshape(NL, 4, 128).transpose(0, 2, 1))
    m["ggla"] = col4(inputs["gla_norm_g"])
    are, aim, ldt = inputs["s5_a_re"][:NL], inputs["s5_a_im"][:NL], inputs["s5_log_dt"][:NL]
    ldt_e = np.broadcast_to(ldt[:, :, None], are.shape)
    tocol = lambda a: a.reshape(NL, 16, 2, 64).transpose(0, 2, 3, 1).reshape(NL, 128, 16)
    m["s5col"] = f(np.stack([tocol(are), tocol(aim), tocol(ldt_e)], axis=2))
    torep = lambda a: np.broadcast_to(a.reshape(NL, 1, 2048), (NL, 128, 2048))
    m["s5rep"] = f(np.stack([torep(are), torep(aim), torep(ldt_e)], axis=2))
    bt = np.zeros((NL, 2, 128, 16, 128), np.float32)
    cp = np.zeros((NL, 2, 128, 16, 128), np.float32)
    for pl, (bsrc, csrc) in enumerate(((inputs["s5_b_re"], inputs["s5_c_re"]), (inputs["s5_b_im"], inputs["s5_c_im"]))):
        for j in range(16):
            ip = j % 4
            for g2 in range(2):
                g = 2 * j + g2
                r0 = ip * 32 + g2 * 16
                bt[:, pl, r0:r0 + 16, j, g2 * 64:(g2 + 1) * 64] = bsrc[:NL, g].transpose(0, 2, 1)
                cp[:, pl, g2 * 64:(g2 + 1) * 64, j, r0:r0 + 16] = csrc[:NL, g].transpose(0, 2, 1)
    m["btpad"] = bt.reshape(NL, 2, 128, 2048)
    m["cpad"] = cp.reshape(NL, 2, 128, 2048)
    m["s5d"] = col4(inputs["s5_d"])
    m["wglu"] = f(inputs["s5_w_glu"][:NL])
    m["bglu"] = col4(inputs["s5_b_glu"])
    ii = np.arange(128)
    m["c_triu"] = (ii[:, None] <= ii[None, :]).astype(np.float32)
    m["c_trils"] = (ii[:, None] > ii[None, :]).astype(np.float32)
    m["c_masku4"] = np.tile(m["c_triu"], (1, 4)).astype(np.float32)
    slopes = _alibi_slopes(16)
    sm = np.zeros((128, 8, 4, 128), np.float64)
    s_ = ii[:, None].astype(np.float64); t_ = ii[None, :].astype(np.float64)
    for j in range(8):
        hA, hB = swa_tile_heads(j)
        for idx in range(4):
            h = hA if idx % 2 == 0 else hB
            if idx < 2:
                dist = t_ - s_
                valid = dist >= 0
            else:
                dist = t_ - s_ + 128
                valid = dist < 128
            sm[:, j, idx, :] = np.where(valid, np.exp(-slopes[h] * dist), 0.0)
    m["c_swam"] = sm.reshape(128, -1).astype(np.float32)
    m["c_iota"] = np.broadcast_to((ii + 1).astype(np.float32)[None, :], (128, 128)).copy()
    oz = np.ones((128, 128), np.float32); oz[:, 0] = 0.0
    m["c_onesz"] = oz
    return m


_CACHE = {}


def kernel(**inputs):
    T = SEQ
    if T not in _CACHE:
        _CACHE[T] = build(T, 8)
    nc = _CACHE[T]
    inputs = {k: np.asarray(v) for k, v in inputs.items()}
    in_maps = [host_prep(inputs, i // 2, T, 1, l0=i % 2, role=i % 2) for i in range(8)]
    res = run_bass_kernel_spmd(nc, in_maps, core_ids=list(range(8)))
    out = np.stack([res.results[2 * b + 1]["outT"][:, LAG * 128:].T for b in range(4)], axis=0)
    return np.ascontiguousarray(out, dtype=np.float32)
```
